# Optimizing a Trainium2 kernel written in Bass

```python
import jax, jax.numpy as jnp
from jax import lax
import numpy as np

D_MODEL = 2048
BATCH = 16
SEQ = 256
DEPTH = 1
DEC_BATCH = 4
DEC_SEQ = 2048
PAST_LEN = 512

GRID_W = 64
D_MIX = D_MODEL
D_RG = D_MIX // 2
D_CH = D_MIX - D_RG
N_RG_HEADS = 8
RG_HEAD = D_RG // N_RG_HEADS
N_CH_HEADS = 8
CH_HEAD = D_CH // N_CH_HEADS
CHUNK = 128
ROWS_PER_CHUNK = CHUNK // GRID_W
CONV_W = 4
CONV_LEFT = 2
RG_C = 8.0
D_FF = 4 * D_MODEL
N_MOD = 6
D_IN = 2 * D_RG + 2 * D_CH
EPS = 1e-6

kernel_name = "hybrid_rglru_chunkmlp_diffusion_step"


def _rmsnorm(x, g):
    xf = x.astype(jnp.float32)
    y = xf * lax.rsqrt(jnp.mean(xf * xf, axis=-1, keepdims=True) + EPS)
    return (y * g.astype(jnp.float32)).astype(x.dtype)


def _modulation(cond, w_ada, b_ada):
    m = jax.nn.silu(cond) @ w_ada + b_ada
    return jnp.split(m[:, None, :], N_MOD, axis=-1)


def _centred_dwconv(x, w, b):
    L = x.shape[1]
    xp = jnp.pad(x, ((0, 0), (CONV_LEFT, CONV_W - 1 - CONV_LEFT), (0, 0)))
    out = b
    for k in range(CONV_W):
        out = out + w[k] * xp[:, k:k + L]
    return out


def _block_diag(x, w, b):
    B, L, _ = x.shape
    xh = x.reshape(B, L, N_RG_HEADS, RG_HEAD)
    return jnp.einsum('blhi,hij->blhj', xh, w.astype(jnp.float32)).reshape(B, L, D_RG) + b.astype(jnp.float32)


def _linear_scan(a, bx, h0, reverse):
    def step(h, inp):
        a_t, b_t = inp
        h = a_t * h + b_t
        return h, h
    h_last, hs = lax.scan(step, h0, (jnp.swapaxes(a, 0, 1), jnp.swapaxes(bx, 0, 1)), reverse=reverse)
    return jnp.swapaxes(hs, 0, 1), h_last


def _rglru_dir(xf, h0, wa, ba, wi, bi, lam, reverse):
    r = jax.nn.sigmoid(_block_diag(xf, wa, ba))
    i = jax.nn.sigmoid(_block_diag(xf, wi, bi))
    log_a = -RG_C * r * jax.nn.softplus(-lam.astype(jnp.float32))
    a = jnp.exp(log_a)
    mult = jnp.sqrt(jnp.maximum(-jnp.expm1(2.0 * log_a), 0.0))
    return _linear_scan(a, mult * (i * xf), h0, reverse)


def _chunk_sgu(u, v, w_s, b_s, n_chunks):
    B, L, _ = v.shape
    vh = v.reshape(B, n_chunks, CHUNK, N_CH_HEADS, CH_HEAD)
    mixed = jnp.einsum('hpq,bnqhd->bnphd', w_s, vh) + jnp.swapaxes(b_s, 0, 1)[None, None, :, :, None]
    return u * mixed.reshape(B, L, D_CH)


def _layer(x, cond, h0, n_chunks, p):
    shift1, scale1, gate1, shift2, scale2, gate2 = _modulation(cond, p['w_ada'], p['b_ada'])
    h = _rmsnorm(x, p['norm1_g']) * (1.0 + scale1) + shift1
    proj = h @ p['w_in']
    y_rg = proj[..., :D_RG]
    x_rg = proj[..., D_RG:2 * D_RG]
    uv = jax.nn.gelu(proj[..., 2 * D_RG:])
    xc = _centred_dwconv(x_rg, p['conv_w'], p['conv_b']).astype(jnp.float32)
    hf, hf_last = _rglru_dir(xc, h0[:, 0], p['ga_w'][0], p['ga_b'][0], p['gi_w'][0], p['gi_b'][0], p['lam'][0], False)
    hb, hb_last = _rglru_dir(xc, h0[:, 1], p['ga_w'][1], p['ga_b'][1], p['gi_w'][1], p['gi_b'][1], p['lam'][1], True)
    rg_out = jax.nn.gelu(y_rg) * (hf + hb).astype(x.dtype)
    ch_out = _chunk_sgu(uv[..., :D_CH], uv[..., D_CH:], p['sgu_w'], p['sgu_b'], n_chunks)
    mix = jnp.concatenate([rg_out, ch_out], axis=-1) @ p['w_out']
    x = x + gate1 * mix
    h2 = _rmsnorm(x, p['norm2_g']) * (1.0 + scale2) + shift2
    ff = jnp.square(jax.nn.relu(h2 @ p['w_ff1'])) @ p['w_ff2']
    x = x + gate2 * ff
    return x, jnp.stack([hf_last, hb_last], axis=1)


def setup_inputs(seed: int = 0) -> dict:
    key = jax.random.key(seed)
    ks = jax.random.split(key, 24)
    f32 = jnp.float32
    nrm = lambda k, shape, s: jax.random.normal(k, shape, f32) * s
    u = jax.random.uniform(ks[14], (DEPTH, 2, D_RG), f32, 0.9, 0.999)
    s = u ** (1.0 / RG_C)
    lam = jnp.log(s) - jnp.log1p(-s)
    return {
        'x_prompt': nrm(ks[0], (BATCH, SEQ, D_MODEL), 1.0),
        'x_sample': nrm(ks[1], (DEC_BATCH, DEC_SEQ, D_MODEL), 1.0),
        'c': nrm(ks[2], (DEC_BATCH, D_MODEL), 1.0),
        'state_rglru': nrm(ks[3], (DEC_BATCH, DEPTH, 2, D_RG), 0.5),
        'c_ctx': nrm(ks[4], (D_MODEL,), 1.0),
        'norm1_g': 1.0 + nrm(ks[5], (DEPTH, D_MODEL), 0.02),
        'w_ada': nrm(ks[6], (DEPTH, D_MODEL, N_MOD * D_MODEL), 0.5 * D_MODEL ** -0.5),
        'b_ada': nrm(ks[7], (DEPTH, N_MOD * D_MODEL), 0.02),
        'w_in': nrm(ks[8], (DEPTH, D_MODEL, D_IN), D_MODEL ** -0.5),
        'conv_w': nrm(ks[9], (DEPTH, CONV_W, D_RG), CONV_W ** -0.5),
        'conv_b': nrm(ks[10], (DEPTH, D_RG), 0.02),
        'ga_w': nrm(ks[11], (DEPTH, 2, N_RG_HEADS, RG_HEAD, RG_HEAD), RG_HEAD ** -0.5),
        'ga_b': nrm(ks[12], (DEPTH, 2, D_RG), 0.02),
        'gi_w': nrm(ks[13], (DEPTH, 2, N_RG_HEADS, RG_HEAD, RG_HEAD), RG_HEAD ** -0.5),
        'gi_b': nrm(ks[15], (DEPTH, 2, D_RG), 0.02),
        'lru_lambda': lam,
        'sgu_w': nrm(ks[16], (DEPTH, N_CH_HEADS, CHUNK, CHUNK), CHUNK ** -0.5),
        'sgu_b': nrm(ks[17], (DEPTH, N_CH_HEADS, CHUNK), 0.02),
        'w_out': nrm(ks[18], (DEPTH, D_MIX, D_MODEL), D_MIX ** -0.5),
        'norm2_g': 1.0 + nrm(ks[19], (DEPTH, D_MODEL), 0.02),
        'w_ff1': nrm(ks[20], (DEPTH, D_MODEL, D_FF), D_MODEL ** -0.5),
        'w_ff2': nrm(ks[21], (DEPTH, D_FF, D_MODEL), D_FF ** -0.5),
        'final_g': 1.0 + nrm(ks[22], (D_MODEL,), 0.02),
    }


def reference(x_prompt, x_sample, c, state_rglru, c_ctx, norm1_g, w_ada, b_ada, w_in,
              conv_w, conv_b, ga_w, ga_b, gi_w, gi_b, lru_lambda, sgu_w, sgu_b, w_out,
              norm2_g, w_ff1, w_ff2, final_g):
    b_ctx, l_ctx = x_prompt.shape[0], x_prompt.shape[1]
    ctx_chunks = l_ctx // CHUNK
    rows = x_sample.shape[1] // GRID_W
    lat_chunks = rows // ROWS_PER_CHUNK
    cond_ctx = jnp.broadcast_to(c_ctx, (b_ctx, D_MODEL))
    xp = x_prompt
    xs = x_sample
    ctx_states = []
    for l in range(DEPTH):
        p = {'w_ada': w_ada[l], 'b_ada': b_ada[l], 'norm1_g': norm1_g[l], 'w_in': w_in[l],
             'conv_w': conv_w[l], 'conv_b': conv_b[l], 'ga_w': ga_w[l], 'ga_b': ga_b[l],
             'gi_w': gi_w[l], 'gi_b': gi_b[l], 'lam': lru_lambda[l], 'sgu_w': sgu_w[l],
             'sgu_b': sgu_b[l], 'w_out': w_out[l], 'norm2_g': norm2_g[l],
             'w_ff1': w_ff1[l], 'w_ff2': w_ff2[l]}
        h0_ctx = jnp.zeros((b_ctx, 2, D_RG), jnp.float32)
        xp, st = _layer(xp, cond_ctx, h0_ctx, ctx_chunks, p)
        ctx_states.append(st)
        xs, _ = _layer(xs, c, state_rglru[:, l].astype(jnp.float32), lat_chunks, p)
    y_prompt = _rmsnorm(xp, final_g)
    y_sample = _rmsnorm(xs, final_g)
    new_state_rglru = jnp.stack(ctx_states, axis=1).astype(x_prompt.dtype)
    return (y_prompt, y_sample, new_state_rglru)
```

```python
import numpy as np
from contextlib import ExitStack
import concourse.bass as bass
import concourse.mybir as mybir
from concourse.bass_utils import run_bass_kernel_spmd

F32 = mybir.dt.float32
BF16 = mybir.dt.bfloat16
AF = mybir.ActivationFunctionType
ALU = mybir.AluOpType

DM = 2048
NOWN = 1536
NOTH = 1024
EPOCH = 2000
SB_BASE = 16512
SB_END = 229344

PK_COND, PK_G1, PK_G2, PK_BADA, PK_CONVW, PK_CONVB = 0, 32, 48, 64, 160, 192
PK_GAB, PK_GIB, PK_LAM, PK_H0, PK_SEL, PK_IDENT, NPK = 200, 216, 232, 248, 264, 272, 400


class Buf:
    __slots__ = ("name", "w", "r", "dsem", "dcnt", "const")

    def __init__(self, name, const=False):
        self.name = name
        self.w = None
        self.r = {}
        self.dsem = None
        self.dcnt = 0
        self.const = const


class Tracker:
    ENGS = ("pe", "act", "dve", "pool", "sp")

    def __init__(self, nc, es):
        self.nc = nc
        self.es = es
        self.q = {e: [] for e in self.ENGS}
        self.sems = []
        self.cur = {}
        self.waited = {e: {} for e in self.ENGS}
        self.last = {}
        self.dma_toks = []

    def new_sem(self, name):
        h = self.es.enter_context(self.nc.semaphore(f"{name}{len(self.sems)}"))
        self.sems.append(h)
        return len(self.sems) - 1

    def _tick(self, eng):
        c = self.cur.get(eng)
        if c is None or c[1] >= EPOCH:
            c = [self.new_sem("c" + eng), 0]
            self.cur[eng] = c
        c[1] += 1
        tok = (c[0], c[1], eng)
        self.last[eng] = tok
        return tok

    def need(self, eng, tok):
        if tok is None:
            return
        sid, val, src = tok
        if eng == "pe" and src == "pe":
            return
        if self.waited[eng].get(sid, 0) >= val:
            return
        self.waited[eng][sid] = val
        h = self.sems[sid]
        self.q[eng].append(lambda E, h=h, v=val: E.wait_ge(h, v))

    def _deps(self, eng, reads, writes):
        for b in reads:
            self.need(eng, b.w)
        for b in writes:
            self.need(eng, b.w)
            for t in b.r.values():
                self.need(eng, t)

    def _mark(self, tok, reads, writes):
        for b in reads:
            if not b.const:
                b.r[tok[0]] = tok
        for b in writes:
            b.w = tok
            b.r = {}

    def op(self, eng, fn, reads=(), writes=()):
        self._deps(eng, reads, writes)
        tok = self._tick(eng)
        h = self.sems[tok[0]]
        self.q[eng].append(lambda E, fn=fn, h=h: fn(E).then_inc(h, 1))
        self._mark(tok, reads, writes)
        return tok

    def mm(self, mms, reads, writes, transpose=False):
        self._deps("pe", reads, writes)
        tok = self._tick("pe")
        h = self.sems[tok[0]]

        def thunk(E, mms=mms, h=h):
            ins = None
            for m in mms:
                if transpose:
                    ins = E.transpose(out=m[0], in_=m[1], identity=m[2])
                else:
                    ins = E.matmul(m[0], lhsT=m[1], rhs=m[2], start=m[3], stop=m[4])
            ins.then_inc(h, 1)

        self.q["pe"].append(thunk)
        self._mark(tok, reads, writes)
        return tok

    def dma(self, q, out, in_, semb, reads=(), writes=()):
        self._deps(q, reads, writes)
        if semb.dsem is None:
            semb.dsem = self.new_sem("d")
        semb.dcnt += 16
        tok = (semb.dsem, semb.dcnt, "dma")
        h = self.sems[semb.dsem]
        self.q[q].append(lambda E, o=out, i=in_, h=h: E.dma_start(out=o, in_=i).then_inc(h, 16))
        self._mark(tok, reads, writes)
        self.dma_toks.append(tok)
        return tok

    def barrier(self):
        toks = list(self.last.values()) + self.dma_toks
        for e in self.ENGS:
            for t in toks:
                self.need(e, t)
        self.dma_toks = []

    def finish(self):
        for t in self.dma_toks:
            self.need("sp", t)
        for t in self.last.values():
            self.need("sp", t)


def rev(ap):
    n = ap.ap[-1][1]
    return bass.AP(ap.tensor, ap.offset + (n - 1), [[ap.ap[0][0], ap.ap[0][1]], [-1, n]])


def bc_mid(ap, k):
    return bass.AP(ap.tensor, ap.offset, [[ap.ap[0][0], ap.ap[0][1]], [0, k], [ap.ap[1][0], ap.ap[1][1]]])


def bc_last(ap, k):
    return bass.AP(ap.tensor, ap.offset, [[ap.ap[0][0], ap.ap[0][1]], [ap.ap[1][0], ap.ap[1][1]], [0, k]])


def build_nc(debug=False):
    nc = bass.Bass("TRN2", target_bir_lowering=False)
    es = ExitStack()

    def din(name, shape):
        return nc.dram_tensor(name, list(shape), F32, kind="ExternalInput").ap()

    x_own = din("x_own", [NOWN, DM])
    x_oth = din("x_oth", [NOTH, DM])
    pk_d = din("pk", [128, NPK])
    ident_d = din("ident", [128, 128])
    sgub_d = din("sgub", [8, 128])
    fg_d = din("fg", [1, DM])
    w_ada = din("w_ada", [DM, 6 * DM])
    w_in = din("w_in", [DM, 4096])
    w_out = din("w_out", [DM, DM])
    w_ff1 = din("w_ff1", [DM, 8192])
    w_ff2 = din("w_ff2", [8192, DM])
    ga_w = din("ga_w", [16, 128, 128])
    gi_w = din("gi_w", [16, 128, 128])
    sgu_wT = din("sgu_wT", [8, 128, 128])
    y_own = nc.dram_tensor("y_own", [NOWN, DM], F32, kind="ExternalOutput").ap()
    st_d = nc.dram_tensor("st", [128, 32], F32, kind="ExternalOutput").ap()
    x1s = nc.dram_tensor("x1s", [NOWN, DM], F32, kind="Internal").ap()
    dbg = {}
    if debug:
        dbg["modT"] = nc.dram_tensor("dbg_modT", [128, 192], F32, kind="ExternalOutput").ap()
        dbg["hsum"] = nc.dram_tensor("dbg_hsum", [128, 8 * NOWN], F32, kind="ExternalOutput").ap()
        dbg["hT"] = nc.dram_tensor("dbg_hT", [128, 16 * NOWN], BF16, kind="ExternalOutput").ap()
        dbg["cat"] = nc.dram_tensor("dbg_cat", [128, 16 * NOWN], BF16, kind="ExternalOutput").ap()

    T = Tracker(nc, es)

    pos = [SB_BASE]
    cnt = [0]

    def at(off, shape, dt):
        cnt[0] += 1
        return nc.alloc_sbuf_tensor_at(f"t{cnt[0]}", list(shape), dt, offset=off)

    def bump(nbytes):
        o = pos[0]
        pos[0] += (nbytes + 31) // 32 * 32
        return o

    Z0 = bump(49152)
    Z1 = bump(49152)
    Z2 = bump(40960)
    WS = [bump(16384) for _ in range(3)]

    def pers(shape, dt):
        n = int(np.prod(shape[1:])) * (4 if dt == F32 else 2)
        return at(bump(n), shape, dt)

    pk = pers([128, NPK], F32)
    modT = pers([128, 192], F32)
    s1 = pers([128, 32], F32)
    s2 = pers([128, 32], F32)
    sc_b = pers([128, 32], BF16)
    identb = pers([128, 128], BF16)
    ones_f = pers([128, 128], F32)
    GAW_OFF = pos[0]
    gaw_b = pers([128, 16, 128], BF16)
    giw_b = pers([128, 16, 128], BF16)
    cst = pers([128, 8], F32)
    c1h = pers([128, 16], F32)
    c1f = pers([128, 16], F32)
    hba = pers([128, 16], F32)
    hbi = pers([128, 16], F32)
    sptmp = pers([128, 16], F32)
    stat_ss = pers([128, 32], F32)
    stat_sq = pers([128, 32], F32)
    stat_rs = pers([128, 32], F32)
    inits = pers([128, 16], F32)
    itmp = pers([128, 16], F32)
    bown = pers([128, 8, 4], F32)
    both = pers([128, 8, 4], F32)
    st_t = pers([128, 32], F32)
    WST_OFF = pos[0]
    wsT_b = pers([128, 8, 128], BF16)
    sgub_bc = pers([128, 8, 128], F32)
    assert pos[0] <= SB_END, pos[0]

    wslot = [at(WS[i], [128, 16, 512], BF16) for i in range(3)]
    wslot_b = [Buf(f"ws{i}") for i in range(3)]

    tp = [nc.alloc_psum_tensor(f"tp{i}", [128, 1024], BF16) for i in range(2)]
    tp_b = [Buf(f"tp{i}") for i in range(2)]
    mT = nc.alloc_psum_tensor("mT", [128, 512], F32)
    _mT_all = Buf("mT")
    mT_b = [_mT_all] * 6
    NG = 5
    pg = [nc.alloc_psum_tensor(f"pg{i}", [128, 512], F32) for i in range(NG)]
    pg_b = [Buf(f"pg{i}") for i in range(NG)]
    pgi = [0]

    def nextpg():
        i = pgi[0] % NG
        pgi[0] += 1
        return pg[i], pg_b[i]

    pk_b = Buf("pk")
    cb = Buf("cb")
    modT_b = [Buf(f"mod{v}") for v in range(6)]
    s1_b, s2_b = Buf("s1"), Buf("s2")

    def pkc(c0, n):
        return pk[:, c0:c0 + n]

    def act(out, in_, func, reads, writes, bias=None, scale=None, accum=None):
        def fn(E):
            kw = {}
            if bias is not None:
                kw["bias"] = bias
            if scale is not None:
                kw["scale"] = scale
            if accum is not None:
                kw["accum_out"] = accum
            return E.activation(out=out, in_=in_, func=func, **kw)
        return T.op("act", fn, reads, writes)

    def ts(out, in0, s1_, s2_, op0, op1, reads, writes, eng="dve"):
        if s2_ is None:
            return T.op(eng, lambda E: E.tensor_scalar(out=out, in0=in0, scalar1=s1_, scalar2=None, op0=op0), reads, writes)
        return T.op(eng, lambda E: E.tensor_scalar(out=out, in0=in0, scalar1=s1_, scalar2=s2_, op0=op0, op1=op1), reads, writes)

    def tt(out, in0, in1, op, reads, writes, eng="dve"):
        return T.op(eng, lambda E: E.tensor_tensor(out=out, in0=in0, in1=in1, op=op), reads, writes)

    def stt(out, in0, scalar, in1, op0, op1, reads, writes):
        return T.op("dve", lambda E: E.scalar_tensor_tensor(out=out, in0=in0, scalar=scalar, in1=in1, op0=op0, op1=op1), reads, writes)

    def scan(out, d0, d1, init, reads, writes):
        return T.op("dve", lambda E: E.tensor_tensor_scan(out=out, data0=d0, data1=d1, initial=init, op0=ALU.mult, op1=ALU.add), reads, writes)

    def colbc(col_ap, n):
        return bass.AP(col_ap.tensor, col_ap.offset, [[col_ap.ap[0][0], col_ap.ap[0][1]], [0, n]])

    def memset(ap, val, writes, eng="dve"):
        return T.op(eng, lambda E: E.memset(ap, val), (), writes)

    def wsrc(w, r0, c0):
        return w[r0:r0 + 2048, c0:c0 + 512].rearrange("(c p) n -> p c n", p=128)

    stream = []
    for jb in range(8):
        stream.append(("ada", jb, wsrc(w_ada, 0, jb * 512)))
    for blk in (0, 1, 2, 3):
        stream.append(("in", blk, wsrc(w_in, 0, blk * 512)))
    for jb in range(8, 24):
        stream.append(("ada", jb, wsrc(w_ada, 0, jb * 512)))
    for blk in (6, 7, 4, 5):
        stream.append(("in", blk, wsrc(w_in, 0, blk * 512)))
    for db in range(4):
        stream.append(("out", db, wsrc(w_out, 0, db * 512)))
    for g in range(3):
        for fb in range(16):
            stream.append(("ff1", fb, wsrc(w_ff1, 0, fb * 512)))
        for db in range(4):
            for ks in range(4):
                stream.append(("ff2", (db, ks), wsrc(w_ff2, ks * 2048, db * 512)))
    st_issue = [0]
    st_take = [0]
    free_slots = [0, 1, 2]
    loaded = {}

    def ws_issue():
        while free_slots and st_issue[0] < len(stream):
            s = free_slots.pop(0)
            n = st_issue[0]
            st_issue[0] += 1
            T.dma("pool", wslot[s][:, :, :], stream[n][2], wslot_b[s], (), [wslot_b[s]])
            loaded[n] = s

    def ws_acquire(kind, key):
        n = st_take[0]
        assert stream[n][0] == kind and stream[n][1] == key, (stream[n][:2], kind, key)
        st_take[0] += 1
        ws_issue()
        assert n in loaded, "weight ring deadlock"
        return loaded.pop(n)

    def ws_release(s):
        free_slots.append(s)
        ws_issue()

    T.dma("sp", pk[:, :], pk_d[:, :], pk_b, (), [pk_b])
    sg_b = Buf("sgub")
    ci_b, cg_b, ch_b, cw_b = Buf("ci"), Buf("cg"), Buf("ch"), Buf("cw")
    T.dma("pool", identb[:, :], ident_d[:, :], ci_b, (), [ci_b])
    T.dma("pool", gaw_b[:, :, :], ga_w.rearrange("g i j -> i g j"), cg_b, (), [cg_b])
    T.dma("pool", giw_b[:, :, :], gi_w.rearrange("g i j -> i g j"), ch_b, (), [ch_b])
    ws_issue()
    ms_b = Buf("ms")
    memset(ones_f[:, :], 1.0, [ms_b])
    memset(cst[:, 0:1], 1e-6, [ms_b])
    memset(cst[:, 1:2], 0.25, [ms_b])
    memset(cst[:, 2:3], 1.0, [ms_b])
    memset(cst[:, 3:4], 0.0, [ms_b])
    memset(cst[:, 4:5], -0.25, [ms_b])
    memset(st_t[:, :], 0.0, [ms_b])
    identf = pkc(PK_IDENT, 128)

    scb_b = Buf("scb")
    act(sc_b[:, :], pkc(PK_COND, 32), AF.Silu, [pk_b], [scb_b])

    def mod_block(jb):
        s = ws_acquire("ada", jb)
        v = jb // 4
        for js in range(4):
            jj = jb * 4 + js
            mms = [(mT[:, jj * 2:jj * 2 + 2], wslot[s][:, c, js * 128:(js + 1) * 128], sc_b[:, 2 * c:2 * c + 2], c == 0, c == 15)
                   for c in range(16)]
            T.mm(mms, [wslot_b[s], scb_b], [mT_b[v]])
        ws_release(s)
        if jb % 4 == 3:
            o = modT[:, v * 32:(v + 1) * 32].rearrange("p (c e) -> p c e", e=2)
            i0 = mT[:, v * 32:(v + 1) * 32].rearrange("p (c e) -> p c e", e=2)
            i1 = bc_last(pkc(PK_BADA + v * 16, 16), 2)
            tt(o, i0, i1, ALU.add, [mT_b[v], pk_b], [modT_b[v]])
            if v == 1:
                stt(s1[:, :].rearrange("p (c e) -> p c e", e=2), modT[:, 32:64].rearrange("p (c e) -> p c e", e=2), 1.0,
                    bc_last(pkc(PK_G1, 16), 2), ALU.add, ALU.mult, [modT_b[1], pk_b], [s1_b])
            if v == 4:
                stt(s2[:, :].rearrange("p (c e) -> p c e", e=2), modT[:, 128:160].rearrange("p (c e) -> p c e", e=2), 1.0,
                    bc_last(pkc(PK_G2, 16), 2), ALU.add, ALU.mult, [modT_b[4], pk_b], [s2_b])

    def norm_p1(xt_ap, xt_b, junk_ap, junk_b, xn_ap, xn_b, col):
        ss = stat_ss[:, col:col + 1]
        sq = stat_sq[:, col:col + 1]
        rs = stat_rs[:, col:col + 1]
        stb = Buf("st")
        act(junk_ap, xt_ap, AF.Square, [xt_b], [junk_b, stb], accum=ss)
        act(sq, ss, AF.Sqrt, [stb, ms_b], [stb], bias=cst[:, 0:1], scale=1.0 / DM)
        T.op("dve", lambda E: E.reciprocal(out=rs, in_=sq), [stb], [stb])
        tt(xn_ap, xt_ap, colbc(rs, DM), ALU.mult, [xt_b, stb, junk_b], [xn_b])

    def norm_p2(xn_ap, xn_b, sc_t, sc_buf, sh_ap_fn, sh_buf, e, dst_fn, dst_b):
        for hf in range(2):
            mms = [(tp[hf][:, k * 128:(k + 1) * 128], xn_ap[:, (hf * 8 + k) * 128:(hf * 8 + k + 1) * 128], identb[:, :]) for k in range(8)]
            T.mm(mms, [xn_b, ci_b], [tp_b[hf]], transpose=True)
            for k in range(8):
                c = hf * 8 + k
                src = tp[hf][:, k * 128:(k + 1) * 128]
                scl = sc_t[:, 2 * c + e:2 * c + e + 1]
                shf = sh_ap_fn(c, e)
                if k % 2 == 0:
                    act(dst_fn(c), src, AF.Identity, [tp_b[hf], sc_buf, sh_buf], [dst_b], bias=shf, scale=scl)
                else:
                    ts(dst_fn(c), src, scl, shf, ALU.mult, ALU.add, [tp_b[hf], sc_buf, sh_buf], [dst_b])

    def norm_tile(xt_ap, xt_b, junk_ap, junk_b, xn_ap, xn_b, col, sc_t, sc_buf, sh_ap_fn, sh_buf, e, dst_fn, dst_b):
        norm_p1(xt_ap, xt_b, junk_ap, junk_b, xn_ap, xn_b, col)
        norm_p2(xn_ap, xn_b, sc_t, sc_buf, sh_ap_fn, sh_buf, e, dst_fn, dst_b)

    hT_own = at(Z0, [128, 16, NOWN], BF16)
    hT_oth = at(Z1, [128, 16, NOTH], BF16)
    hT_b = [Buf(f"hT{i}") for i in range(20)]
    xt = [at(Z0 + i * 8192, [128, DM], F32) for i in range(3)]
    xt_b = [Buf(f"xt{i}") for i in range(3)]
    junk = at(Z0 + 24576, [128, DM], BF16)
    junk_b = Buf("junk")
    xns = [at(Z1 + 8192 + i * 4096, [128, DM], BF16) for i in range(20)]
    xns_b = [Buf(f"xns{i}") for i in range(20)]

    def hT_ap(c, t0, n):
        if t0 < NOWN:
            assert t0 + n <= NOWN
            return hT_own[:, c, t0:t0 + n]
        return hT_oth[:, c, t0 - NOWN:t0 - NOWN + n]

    def n1_load(i):
        src = x_own[i * 128:(i + 1) * 128, :] if i < 12 else x_oth[(i - 12) * 128:(i - 11) * 128, :]
        T.dma("sp", xt[i % 3][:, :], src, xt_b[i % 3], (), [xt_b[i % 3]])

    n1_load(0)
    n1_load(1)
    for i in range(20):
        if i + 2 < 20:
            n1_load(i + 2)
        norm_p1(xt[i % 3][:, :], xt_b[i % 3], junk[:, :], junk_b, xns[i][:, :], xns_b[i], i)
    for jb in range(8):
        mod_block(jb)
    T.barrier()
    for i in range(20):
        e = 0 if i < 4 else 1
        norm_p2(xns[i][:, :], xns_b[i], s1, s1_b, lambda c, e: modT[:, 2 * c + e:2 * c + e + 1], modT_b[0], e,
                lambda c, i=i: hT_ap(c, i * 128, 128), hT_b[i])
    T.barrier()

    gy = at(Z1 + 65536, [128, 8, NOWN], BF16)
    gy_b = [Buf(f"gy{h}") for h in range(8)]
    for blk in (0, 1):
        s = ws_acquire("in", blk)
        for hh in range(4):
            h = blk * 4 + hh
            for tb in range(3):
                p_, pb_ = nextpg()
                mms = [(p_[:, :], wslot[s][:, c, hh * 128:(hh + 1) * 128], hT_own[:, c, tb * 512:(tb + 1) * 512], c == 0, c == 15) for c in range(16)]
                T.mm(mms, [wslot_b[s]] + hT_b[tb * 4:tb * 4 + 4], [pb_])
                act(gy[:, h, tb * 512:(tb + 1) * 512], p_[:, :], AF.Gelu_apprx_tanh, [pb_], [gy_b[h]])
        ws_release(s)

    prm_b = Buf("prm")
    act(sptmp[:, :], pkc(PK_LAM, 16), AF.Exp, [pk_b], [prm_b], scale=-1.0)
    act(sptmp[:, :], sptmp[:, :], AF.Ln, [prm_b, ms_b], [prm_b], bias=cst[:, 2:3])
    z16 = colbc(cst[:, 3:4], 16)
    stt(c1h[:, :], sptmp[:, :], -4.0, z16, ALU.mult, ALU.add, [prm_b, ms_b], [prm_b])
    stt(c1f[:, :], sptmp[:, :], -8.0, z16, ALU.mult, ALU.add, [prm_b, ms_b], [prm_b])
    stt(hba[:, :], pkc(PK_GAB, 16), 0.5, z16, ALU.mult, ALU.add, [pk_b, ms_b], [prm_b])
    stt(hbi[:, :], pkc(PK_GIB, 16), 0.5, z16, ALU.mult, ALU.add, [pk_b, ms_b], [prm_b])

    s_x2 = ws_acquire("in", 2)
    s_x3 = ws_acquire("in", 3)

    def xw(h, c):
        s = s_x2 if h < 4 else s_x3
        return wslot[s][:, c, (h % 4) * 128:(h % 4 + 1) * 128], wslot_b[s]

    sel0 = pkc(PK_SEL, 1)
    sel1 = pkc(PK_SEL + 1, 1)

    class Unit:
        pass

    def mk_unit(xr_off, xc_offs, xcb_off, set_offs, width, ncomp):
        u = Unit()
        wb = (width * 4 + 31) // 32 * 32
        u.W = width
        u.xr = at(xr_off, [128, width], F32)
        u.xc = [at(o, [128, width], F32) for o in xc_offs]
        u.xcb = at(xcb_off, [128, ncomp], BF16)
        u.xr_b, u.xcb_b = Buf("xr"), Buf("xcb")
        u.xc_b = [Buf("xc0"), Buf("xc1")]
        u.TA = [at(so, [128, width], F32) for so in set_offs]
        u.TI = [at(so + wb, [128, width], F32) for so in set_offs]
        u.MH = [at(so + 2 * wb, [128, width], F32) for so in set_offs]
        u.TA_b = [Buf("TA") for _ in set_offs]
        u.TI_b = [Buf("TI") for _ in set_offs]
        u.MH_b = [Buf("MH") for _ in set_offs]
        memset(u.xr[:, :], 0.0, [u.xr_b])
        for d in range(2):
            memset(u.xc[d][:, :], 0.0, [u.xc_b[d]])
            memset(u.TA[d][:, :], 0.0, [u.TA_b[d]])
            memset(u.TI[d][:, :], 0.0, [u.TI_b[d]])
            memset(u.MH[d][:, :], 0.0, [u.MH_b[d]])
        return u

    def rg_conv(u, h):
        xr, xc, xc_b = u.xr, u.xc[h % 2], u.xc_b[h % 2]
        lo, hi = 2, u.W - 2
        stt(xc[:, lo:hi], xr[:, lo - 2:hi - 2], pkc(PK_CONVW + h * 4, 1), colbc(pkc(PK_CONVB + h, 1), hi - lo), ALU.mult, ALU.add,
            [u.xr_b, pk_b], [xc_b])
        for k in (1, 2, 3):
            stt(xc[:, lo:hi], xr[:, lo - 2 + k:hi - 2 + k], pkc(PK_CONVW + h * 4 + k, 1), xc[:, lo:hi], ALU.mult, ALU.add,
                [u.xr_b, pk_b, xc_b], [xc_b])

    def rg_xcb(u, h, segs):
        for (ps_, ln, cs) in segs:
            act(u.xcb[:, cs:cs + ln], u.xc[h % 2][:, ps_:ps_ + ln], AF.Identity, [u.xc_b[h % 2]], [u.xcb_b])

    def rg_te(u, h, segs):
        lo, hi = 2, u.W - 2
        for d in range(2):
            gi = d * 8 + h
            TA, TI, MH = u.TA[d], u.TI[d], u.MH[d]
            for (ps_, ln, cs) in segs:
                for o in range(0, ln, 512):
                    m = min(512, ln - o)
                    for (gw, hb_, dst, dst_b) in ((gaw_b, hba, TA, u.TA_b[d]), (giw_b, hbi, TI, u.TI_b[d])):
                        p_, pb_ = nextpg()
                        T.mm([(p_[:, 0:m], gw[:, gi, :], u.xcb[:, cs + o:cs + o + m], True, True)], [u.xcb_b, cg_b, ch_b], [pb_])
                        act(dst[:, ps_ + o:ps_ + o + m], p_[:, 0:m], AF.Tanh, [pb_, prm_b], [dst_b],
                            bias=hb_[:, gi:gi + 1], scale=0.5)
            act(MH[:, lo:hi], TA[:, lo:hi], AF.Exp, [u.TA_b[d], prm_b], [u.MH_b[d]], bias=c1f[:, gi:gi + 1], scale=c1f[:, gi:gi + 1])
            act(TA[:, lo:hi], TA[:, lo:hi], AF.Exp, [u.TA_b[d], prm_b], [u.TA_b[d]], bias=c1h[:, gi:gi + 1], scale=c1h[:, gi:gi + 1])

    def rg_tail(u, h, segs, init_f, init_b, init_bufs, out_f, out_b, outf_b, outb_b):
        lo, hi = 2, u.W - 2
        xc, xc_b = u.xc[h % 2], u.xc_b[h % 2]
        for d in range(2):
            stt(u.MH[d][:, lo:hi], u.MH[d][:, lo:hi], 1.0, colbc(cst[:, 4:5], hi - lo), ALU.min, ALU.mult, [u.MH_b[d], ms_b], [u.MH_b[d]])
        for d in range(2):
            act(u.MH[d][:, lo:hi], u.MH[d][:, lo:hi], AF.Sqrt, [u.MH_b[d], ms_b], [u.MH_b[d]], bias=cst[:, 1:2], scale=1.0)
        for d in range(2):
            TA, TI, MH = u.TA[d], u.TI[d], u.MH[d]
            stt(TI[:, lo:hi], TI[:, lo:hi], 1.0, xc[:, lo:hi], ALU.add, ALU.mult, [u.TI_b[d], xc_b], [u.TI_b[d]])
            tt(TI[:, lo:hi], TI[:, lo:hi], MH[:, lo:hi], ALU.mult, [u.TI_b[d], u.MH_b[d]], [u.TI_b[d]])
            for si, (ps_, ln, cs) in enumerate(segs):
                a_ = TA[:, ps_:ps_ + ln]
                b_ = TI[:, ps_:ps_ + ln]
                if d == 0:
                    scan(out_f[:, cs:cs + ln], a_, b_, init_f[si], [u.TA_b[d], u.TI_b[d]] + init_bufs, [outf_b])
                else:
                    scan(rev(out_b[:, cs:cs + ln]), rev(a_), rev(b_), init_b[si], [u.TA_b[d], u.TI_b[d]] + init_bufs, [outb_b])

    def dcopy(out, in_, reads, writes):
        n = out.ap[-1][1]
        return tt(out, in_, colbc(cst[:, 3:4], n), ALU.add, list(reads) + [ms_b], writes)

    OX = Z1 + 32768
    SP0 = pos[0]
    uo = mk_unit(WST_OFF, [OX, WST_OFF + 6176], WST_OFF + 4128, [OX + 4128, OX + 4128 * 4], 1028, 1024)
    assert OX + 4128 * 7 <= Z1 + 65536 and WST_OFF + 6176 + 4128 <= SB_END
    bnd_b = Buf("bnd")
    carry_b = Buf("carry")
    segs_o = [(2, 1024, 0)]

    def oth_front(h):
        p_, pb_ = nextpg()
        mms = []
        for gi_, t0 in enumerate((512, 512 + 1022)):
            for c in range(16):
                mms.append((p_[:, gi_ * 2:gi_ * 2 + 2], xw(h, c)[0], hT_own[:, c, t0:t0 + 2], c == 0, c == 15))
        T.mm(mms, [wslot_b[s_x2], wslot_b[s_x3], hT_b[4], hT_b[11]], [pb_])
        act(bown[:, h, :], p_[:, 0:4], AF.Identity, [pb_], [bnd_b])
        for tb in range(2):
            p_, pb_ = nextpg()
            mms = [(p_[:, :], xw(h, c)[0], hT_oth[:, c, tb * 512:(tb + 1) * 512], c == 0, c == 15) for c in range(16)]
            T.mm(mms, [wslot_b[s_x2], wslot_b[s_x3]] + hT_b[12 + tb * 4:16 + tb * 4], [pb_])
            act(uo.xr[:, 2 + tb * 512:2 + (tb + 1) * 512], p_[:, :], AF.Identity, [pb_], [uo.xr_b])
        stt(uo.xr[:, 0:2], bown[:, h, 2:4], sel0, colbc(cst[:, 3:4], 2), ALU.mult, ALU.add, [bnd_b, pk_b, ms_b], [uo.xr_b])
        stt(uo.xr[:, 1026:1027], bown[:, h, 0:1], sel1, colbc(cst[:, 3:4], 1), ALU.mult, ALU.add, [bnd_b, pk_b, ms_b], [uo.xr_b])
        dcopy(both[:, h, 0:1], uo.xr[:, 2:3], [uo.xr_b], [bnd_b])
        dcopy(both[:, h, 1:3], uo.xr[:, 1024:1026], [uo.xr_b], [bnd_b])
        rg_conv(uo, h)

    oth_front(0)
    rg_xcb(uo, 0, segs_o)
    for h in range(8):
        if h + 1 < 8:
            oth_front(h + 1)
        rg_te(uo, h, segs_o)
        if h + 1 < 8:
            rg_xcb(uo, h + 1, segs_o)
        rg_tail(uo, h, segs_o, [pkc(PK_H0 + h, 1)], [pkc(PK_H0 + 8 + h, 1)], [pk_b],
                uo.MH[0][:, 2:1026], uo.MH[1][:, 2:1026], uo.MH_b[0], uo.MH_b[1])
        stt(itmp[:, 2 * h:2 * h + 1], pkc(PK_H0 + h, 1), sel0, cst[:, 3:4], ALU.mult, ALU.add, [pk_b, ms_b], [carry_b])
        stt(inits[:, 2 * h:2 * h + 1], uo.MH[0][:, 1025:1026], sel1, itmp[:, 2 * h:2 * h + 1], ALU.mult, ALU.add, [uo.MH_b[0], carry_b, pk_b], [carry_b])
        stt(itmp[:, 2 * h + 1:2 * h + 2], pkc(PK_H0 + 8 + h, 1), sel1, cst[:, 3:4], ALU.mult, ALU.add, [pk_b, ms_b], [carry_b])
        stt(inits[:, 2 * h + 1:2 * h + 2], uo.MH[1][:, 2:3], sel0, itmp[:, 2 * h + 1:2 * h + 2], ALU.mult, ALU.add, [uo.MH_b[1], carry_b, pk_b], [carry_b])
        mod_block(8 + h)

    T.barrier()

    W = 1548
    uw = mk_unit(Z1 + 37248, [Z1 + 43456, WST_OFF], Z1 + 49664, [Z1, Z1 + 18624], W, NOWN)
    HF = at(Z1 + 52736, [128, NOWN], F32)
    HB = at(Z1 + 58880, [128, NOWN], F32)
    assert 58880 + 6144 <= 65536 and WST_OFF + 6208 <= SB_END
    HF_b, HB_b = Buf("HF"), Buf("HB")
    st_b = Buf("stt")
    segs_own = [(2, 256, 0), (262, 256, 256), (522, 1024, 512)]

    def own_front(h):
        for tb in range(3):
            p_, pb_ = nextpg()
            mms = [(p_[:, :], xw(h, c)[0], hT_own[:, c, tb * 512:(tb + 1) * 512], c == 0, c == 15) for c in range(16)]
            T.mm(mms, [wslot_b[s_x2], wslot_b[s_x3]] + hT_b[tb * 4:tb * 4 + 4], [pb_])
            if tb == 0:
                act(uw.xr[:, 2:258], p_[:, 0:256], AF.Identity, [pb_], [uw.xr_b])
                act(uw.xr[:, 262:518], p_[:, 256:512], AF.Identity, [pb_], [uw.xr_b])
            else:
                o = 522 + (tb - 1) * 512
                act(uw.xr[:, o:o + 512], p_[:, :], AF.Identity, [pb_], [uw.xr_b])
        stt(uw.xr[:, 520:522], both[:, h, 1:3], sel1, colbc(cst[:, 3:4], 2), ALU.mult, ALU.add, [bnd_b, pk_b, ms_b], [uw.xr_b])
        stt(uw.xr[:, 1546:1547], both[:, h, 0:1], sel0, colbc(cst[:, 3:4], 1), ALU.mult, ALU.add, [bnd_b, pk_b, ms_b], [uw.xr_b])
        rg_conv(uw, h)

    own_front(0)
    rg_xcb(uw, 0, segs_own)
    for h in range(8):
        if h + 1 < 8:
            own_front(h + 1)
        rg_te(uw, h, segs_own)
        if h + 1 < 8:
            rg_xcb(uw, h + 1, segs_own)
        rg_tail(uw, h, segs_own, [0.0, 0.0, inits[:, 2 * h:2 * h + 1]], [0.0, 0.0, inits[:, 2 * h + 1:2 * h + 2]], [carry_b],
                HF, HB, HF_b, HB_b)
        for sq_ in range(2):
            dcopy(st_t[:, (sq_ * 2) * 8 + h:(sq_ * 2) * 8 + h + 1], HF[:, sq_ * 256 + 255:sq_ * 256 + 256], [HF_b], [st_b])
            dcopy(st_t[:, (sq_ * 2 + 1) * 8 + h:(sq_ * 2 + 1) * 8 + h + 1], HB[:, sq_ * 256:sq_ * 256 + 1], [HB_b], [st_b])
        tt(HF[:, :], HF[:, :], HB[:, :], ALU.add, [HF_b, HB_b], [HF_b])
        tt(gy[:, h, :], gy[:, h, :], HF[:, :], ALU.mult, [gy_b[h], HF_b], [gy_b[h]])
        mod_block(16 + h)
    ws_release(s_x2)
    ws_release(s_x3)
    T.dma("sp", st_d[:, :], st_t[:, :], st_b, [st_b], [])
    rg_out = gy
    if debug:
        T.dma("sp", dbg["modT"][:, :], modT[:, :], modT_b[5], [modT_b[v] for v in range(6)], [])
        T.dma("sp", dbg["hT"][:, :], hT_own[:, :, :].rearrange("p c t -> p (c t)"), hT_b[0], hT_b[0:12], [])

    T.barrier()

    T.dma("sp", sgub_bc[:, :, :], bass.AP(sgub_d.tensor, 0, [[0, 128], [128, 8], [1, 128]]), sg_b, (), [sg_b])
    T.dma("pool", wsT_b[:, :, :], sgu_wT.rearrange("h q p -> q h p"), cw_b, (), [cw_b])
    gtmp = [at(Z2 + i * 2048, [128, 512], F32) for i in range(4)]
    gtmp_b = [Buf(f"gt{i}") for i in range(4)]
    gti = [0]

    def next_gt():
        i = gti[0] % 4
        gti[0] += 1
        return gtmp[i], gtmp_b[i]

    vtm = at(Z1, [128, 12, 1024], BF16)
    ch_out = at(Z1 + 24576, [128, 8, NOWN], BF16)
    vt_b = [Buf(f"vt{i}") for i in range(12)]
    ch_b2 = Buf("chout")
    s_v = [ws_acquire("in", 6), ws_acquire("in", 7)]
    for ci in range(12):
        for vb in range(2):
            p_, pb_ = nextpg()
            mms = [(p_[:, :], hT_own[:, c, ci * 128:(ci + 1) * 128], wslot[s_v[vb]][:, c, :], c == 0, c == 15) for c in range(16)]
            T.mm(mms, [wslot_b[s_v[vb]], hT_b[ci]], [pb_])
            act(vtm[:, ci, vb * 512:(vb + 1) * 512], p_[:, :], AF.Gelu_apprx_tanh, [pb_], [vt_b[ci]])
    ws_release(s_v[0])
    ws_release(s_v[1])
    for blk in (4, 5):
        s = ws_acquire("in", blk)
        for hh in range(4):
            h = (blk - 4) * 4 + hh
            for tb in range(3):
                p_, pb_ = nextpg()
                mms = [(p_[:, :], wslot[s][:, c, hh * 128:(hh + 1) * 128], hT_own[:, c, tb * 512:(tb + 1) * 512], c == 0, c == 15) for c in range(16)]
                T.mm(mms, [wslot_b[s]] + hT_b[tb * 4:tb * 4 + 4], [pb_])
                g_, gb_ = next_gt()
                act(g_[:, :], p_[:, :], AF.Gelu_apprx_tanh, [pb_], [gb_])
                pm, pmb = nextpg()
                mms = [(pm[:, k * 128:(k + 1) * 128], vtm[:, tb * 4 + k, h * 128:(h + 1) * 128], wsT_b[:, h, :], True, True) for k in range(4)]
                T.mm(mms, vt_b[tb * 4:tb * 4 + 4] + [cw_b], [pmb])
                m_, mb_ = next_gt()
                tt(m_[:, :].rearrange("p (k q) -> p k q", k=4), pm[:, :].rearrange("p (k q) -> p k q", k=4), bc_mid(sgub_bc[:, h, :], 4),
                   ALU.add, [pmb, sg_b], [mb_])
                tt(ch_out[:, h, tb * 512:(tb + 1) * 512], m_[:, :], g_[:, :], ALU.mult, [mb_, gb_], [ch_b2])
        ws_release(s)
    if debug:
        T.dma("sp", dbg["cat"][:, 0:8 * NOWN], rg_out[:, :, :].rearrange("p c t -> p (c t)"), gy_b[0], gy_b, [])
        T.dma("sp", dbg["cat"][:, 8 * NOWN:16 * NOWN], ch_out[:, :, :].rearrange("p c t -> p (c t)"), ch_b2, [ch_b2], [])

    T.barrier()

    def build_gate(v, e, diag, diag_b, gbc, gbc_b):
        for c in range(16):
            col = (v * 16 + c) * 2 + e
            stt(diag[:, c * 128:(c + 1) * 128], identf, modT[:, col:col + 1], colbc(cst[:, 3:4], 128), ALU.mult, ALU.add,
                [pk_b, modT_b[v], ms_b], [diag_b])
        for q in range(4):
            p_, pb_ = nextpg()
            T.mm([(p_[:, :], ones_f[:, :], diag[:, q * 512:(q + 1) * 512], True, True)], [diag_b, ms_b], [pb_])
            act(gbc[:, q * 512:(q + 1) * 512], p_[:, :], AF.Identity, [pb_], [gbc_b])

    gbc1 = [at(Z0 + e * 8192, [128, DM], F32) for e in range(2)]
    gbc1_b = [Buf("gb1a"), Buf("gb1b")]
    diag1 = at(Z0 + 16384, [128, DM], F32)
    diag1_b = Buf("diag1")
    xblk = [at(Z0 + 24576 + i * 2048, [128, 512], F32) for i in range(4)]
    xblk_b = [Buf(f"xb{i}") for i in range(4)]
    oblk = [at(Z0 + 32768 + i * 2048, [128, 512], F32) for i in range(4)]
    oblk_b = [Buf(f"ob{i}") for i in range(4)]
    x1s_b = [[Buf(f"x1s{i}_{db}") for db in range(4)] for i in range(12)]
    for e in range(2):
        build_gate(2, e, diag1, diag1_b, gbc1[e], gbc1_b[e])

    jobs = [(db, i) for db in range(4) for i in range(12)]

    def b_load(n):
        db, i = jobs[n]
        T.dma("sp", xblk[n % 4][:, :], x_own[i * 128:(i + 1) * 128, db * 512:(db + 1) * 512], xblk_b[n % 4], (), [xblk_b[n % 4]])

    b_load(0)
    b_load(1)
    s = None
    for n, (db, i) in enumerate(jobs):
        if n + 2 < len(jobs):
            b_load(n + 2)
        if i == 0:
            s = ws_acquire("out", db)
        e = 0 if i < 4 else 1
        p_, pb_ = nextpg()
        mms = []
        for c in range(16):
            src = rg_out[:, c, i * 128:(i + 1) * 128] if c < 8 else ch_out[:, c - 8, i * 128:(i + 1) * 128]
            mms.append((p_[:, :], src, wslot[s][:, c, :], c == 0, c == 15))
        T.mm(mms, [wslot_b[s], ch_b2] + gy_b, [pb_])
        ob, obb = oblk[n % 4], oblk_b[n % 4]
        tt(ob[:, :], p_[:, :], gbc1[e][:, db * 512:(db + 1) * 512], ALU.mult, [pb_, gbc1_b[e]], [obb])
        tt(ob[:, :], ob[:, :], xblk[n % 4][:, :], ALU.add, [obb, xblk_b[n % 4]], [obb])
        T.dma("sp", x1s[i * 128:(i + 1) * 128, db * 512:(db + 1) * 512], ob[:, :], obb, [obb], [x1s_b[i][db]])
        if i == 11:
            ws_release(s)

    T.barrier()

    h2T = [at(Z0 + i * 16384, [128, 16, 512], BF16) for i in range(2)]
    h2T_b = [[Buf(f"h2T{i}_{t}") for t in range(4)] for i in range(2)]
    xq = [at(Z0 + 32768 + i * 8192, [128, DM], F32) for i in range(2)]
    xq_b = [Buf("xq0"), Buf("xq1")]
    xqi = [0]
    aT = at(Z1, [128, 64, 512], BF16)
    aT_b = [Buf(f"aT{i}") for i in range(64)]
    gbc2 = at(Z1 + 65536, [128, DM], F32)
    gbc2_b = Buf("gbc2")
    SCR = Z1 + 65536 + 8192
    diag2 = at(SCR, [128, DM], F32)
    xn2 = at(SCR, [128, DM], BF16)
    rtmp = [at(SCR + 4096 + i * 2048, [128, 512], F32) for i in range(2)]
    scr_b = Buf("scr")
    fg_bc = at(SCR + 8192, [128, DM], F32)
    fg_b = Buf("fg")
    assert SCR + 16384 <= Z2 + 40960
    SP0 = pos[0]
    xb = [at(GAW_OFF + i * 2048, [128, 512], F32) for i in range(4)]
    xb_b = [Buf(f"xb{i}") for i in range(4)]
    sqj = at(SP0, [128, 512], BF16)
    sqj_b = Buf("sqj")
    assert SP0 + 1024 <= SB_END
    _fst = Buf("fst")
    fstat_b = [_fst, _fst, _fst]
    T.dma("sp", fg_bc[:, :], bass.AP(fg_d.tensor, 0, [[0, 128], [1, DM]]), fg_b, (), [fg_b])

    def c_norm_p1(g, t_):
        i = g * 4 + t_
        q = xqi[0] % 2
        xqi[0] += 1
        T.dma("sp", xq[q][:, :], x1s[i * 128:(i + 1) * 128, :], xq_b[q], x1s_b[i], [xq_b[q]])
        norm_p1(xq[q][:, :], xq_b[q], xn2[:, :], scr_b, xn2[:, :], scr_b, 20 + t_)

    def c_norm_p2(g, t_):
        e = 0 if g == 0 else 1
        norm_p2(xn2[:, :], scr_b, s2, s2_b, lambda c, e: modT[:, 96 + 2 * c + e:96 + 2 * c + e + 1], modT_b[3], e,
                lambda c, t_=t_, g=g: h2T[g % 2][:, c, t_ * 128:(t_ + 1) * 128], h2T_b[g % 2][t_])

    def c_final(g, t_):
        i = g * 4 + t_
        q = xqi[0] % 2
        xqi[0] += 1
        T.dma("sp", xq[q][:, :], x1s[i * 128:(i + 1) * 128, :], xq_b[q], x1s_b[i], [xq_b[q]])
        stt(xq[q][:, :], xq[q][:, :], stat_rs[:, 8 + g * 4 + t_:8 + g * 4 + t_ + 1], fg_bc[:, :], ALU.mult, ALU.mult,
            [xq_b[q], fstat_b[g], fg_b], [xq_b[q]])
        T.dma("sp", y_own[i * 128:(i + 1) * 128, :], xq[q][:, :], xq_b[q], [xq_b[q]], [Buf("y")])

    for t_ in range(4):
        c_norm_p1(0, t_)
        c_norm_p2(0, t_)
    for g in range(3):
        e = 0 if g == 0 else 1
        if g < 2:
            build_gate(5, e, diag2, scr_b, gbc2, gbc2_b)
        for fb in range(16):
            s = ws_acquire("ff1", fb)
            for js in range(4):
                p_, pb_ = nextpg()
                mms = [(p_[:, :], wslot[s][:, c, js * 128:(js + 1) * 128], h2T[g % 2][:, c, :], c == 0, c == 15) for c in range(16)]
                T.mm(mms, [wslot_b[s]] + h2T_b[g % 2], [pb_])
                r_ = rtmp[(fb * 4 + js) % 2]
                act(r_[:, :], p_[:, :], AF.Relu, [pb_], [scr_b])
                tt(aT[:, fb * 4 + js, :], r_[:, :], r_[:, :], ALU.mult, [scr_b], [aT_b[fb * 4 + js]])
            ws_release(s)
            if g > 0 and fb in (2, 5, 8, 11):
                c_final(g - 1, (fb - 2) // 3)
        for db in range(4):
            if g < 2:
                c_norm_p1(g + 1, db)
            for t_ in range(4):
                i = g * 4 + t_
                T.dma("sp", xb[t_][:, :], x1s[i * 128:(i + 1) * 128, db * 512:(db + 1) * 512], xb_b[t_], [x1s_b[i][db]], [xb_b[t_]])
            banks = [nextpg() for _ in range(4)]
            for ks in range(4):
                s = ws_acquire("ff2", (db, ks))
                for t_ in range(4):
                    p_, pb_ = banks[t_]
                    mms = [(p_[:, :], aT[:, ks * 16 + c, t_ * 128:(t_ + 1) * 128], wslot[s][:, c, :], ks == 0 and c == 0, ks == 3 and c == 15)
                           for c in range(16)]
                    T.mm(mms, [wslot_b[s]] + aT_b[ks * 16:ks * 16 + 16], [pb_])
                ws_release(s)
            if g < 2:
                c_norm_p2(g + 1, db)
            for t_ in range(4):
                i = g * 4 + t_
                p_, pb_ = banks[t_]
                j = t_
                r_ = rtmp[t_ % 2]
                tt(r_[:, :], p_[:, :], gbc2[:, db * 512:(db + 1) * 512], ALU.mult, [pb_, gbc2_b], [scr_b])
                tt(xb[j][:, :], xb[j][:, :], r_[:, :], ALU.add, [scr_b, xb_b[j]], [xb_b[j]])
                act(sqj[:, :], xb[j][:, :], AF.Square, [xb_b[j]], [sqj_b, fstat_b[g]], accum=stat_ss[:, db * 4 + t_:db * 4 + t_ + 1])
                T.dma("sp", x1s[i * 128:(i + 1) * 128, db * 512:(db + 1) * 512], xb[j][:, :], xb_b[j], [xb_b[j]], [x1s_b[i][db]])
        tot = stat_sq[:, 0:4]
        tt(tot, stat_ss[:, 0:4], stat_ss[:, 4:8], ALU.add, [fstat_b[g]], [fstat_b[g]])
        tt(tot, tot, stat_ss[:, 8:12], ALU.add, [fstat_b[g]], [fstat_b[g]])
        tt(tot, tot, stat_ss[:, 12:16], ALU.add, [fstat_b[g]], [fstat_b[g]])
        act(tot, tot, AF.Sqrt, [fstat_b[g], ms_b], [fstat_b[g]], bias=cst[:, 0:1], scale=1.0 / DM)
        T.op("dve", lambda E, g=g, tot=tot: E.reciprocal(out=stat_rs[:, 8 + g * 4:12 + g * 4], in_=tot), [fstat_b[g]], [fstat_b[g]])
    for t_ in range(4):
        c_final(2, t_)
    assert st_take[0] == len(stream)
    T.finish()

    with nc.Block() as block:
        @block.sync
        def _(E):
            for f in T.q["sp"]:
                f(E)

        @block.gpsimd
        def _(E):
            for f in T.q["pool"]:
                f(E)

        @block.scalar
        def _(E):
            for f in T.q["act"]:
                f(E)

        @block.vector
        def _(E):
            for f in T.q["dve"]:
                f(E)

        @block.tensor
        def _(E):
            for f in T.q["pe"]:
                f(E)
    es.close()
    return nc


def _fm(v, n):
    return np.ascontiguousarray(np.asarray(v, np.float32).reshape(n, 128).T)


def make_in_maps(inp):
    x_prompt = np.asarray(inp["x_prompt"], np.float32)
    x_sample = np.asarray(inp["x_sample"], np.float32)
    c = np.asarray(inp["c"], np.float32)
    st = np.asarray(inp["state_rglru"], np.float32)
    shared = {
        "ident": np.eye(128, dtype=np.float32),
        "sgub": np.ascontiguousarray(np.asarray(inp["sgu_b"], np.float32)[0]),
        "fg": np.ascontiguousarray(np.asarray(inp["final_g"], np.float32).reshape(1, DM)),
        "w_ada": np.ascontiguousarray(np.asarray(inp["w_ada"], np.float32)[0]),
        "w_in": np.ascontiguousarray(np.asarray(inp["w_in"], np.float32)[0]),
        "w_out": np.ascontiguousarray(np.asarray(inp["w_out"], np.float32)[0]),
        "w_ff1": np.ascontiguousarray(np.asarray(inp["w_ff1"], np.float32)[0]),
        "w_ff2": np.ascontiguousarray(np.asarray(inp["w_ff2"], np.float32)[0]),
        "ga_w": np.ascontiguousarray(np.asarray(inp["ga_w"], np.float32)[0].reshape(16, 128, 128)),
        "gi_w": np.ascontiguousarray(np.asarray(inp["gi_w"], np.float32)[0].reshape(16, 128, 128)),
        "sgu_wT": np.ascontiguousarray(np.asarray(inp["sgu_w"], np.float32)[0].transpose(0, 2, 1)),
    }
    g1 = _fm(inp["norm1_g"][0], 16)
    g2 = _fm(inp["norm2_g"][0], 16)
    bada = _fm(inp["b_ada"][0], 96)
    convw = np.asarray(inp["conv_w"], np.float32)[0].reshape(4, 8, 128).transpose(2, 1, 0).reshape(128, 32)
    convb = _fm(inp["conv_b"][0], 8)
    gab = np.asarray(inp["ga_b"], np.float32)[0].reshape(2, 8, 128).transpose(2, 0, 1).reshape(128, 16)
    gib = np.asarray(inp["gi_b"], np.float32)[0].reshape(2, 8, 128).transpose(2, 0, 1).reshape(128, 16)
    lam = np.asarray(inp["lru_lambda"], np.float32)[0].reshape(2, 8, 128).transpose(2, 0, 1).reshape(128, 16)
    cctx = _fm(inp["c_ctx"], 16)
    maps = []
    for k in range(8):
        b, half = k // 2, k % 2
        pk = np.zeros((128, NPK), np.float32)
        cond = np.stack([cctx, _fm(c[b], 16)], axis=-1).reshape(128, 32)
        pk[:, PK_COND:PK_COND + 32] = cond
        pk[:, PK_G1:PK_G1 + 16] = g1
        pk[:, PK_G2:PK_G2 + 16] = g2
        pk[:, PK_BADA:PK_BADA + 96] = bada
        pk[:, PK_CONVW:PK_CONVW + 32] = convw
        pk[:, PK_CONVB:PK_CONVB + 8] = convb
        pk[:, PK_GAB:PK_GAB + 16] = gab
        pk[:, PK_GIB:PK_GIB + 16] = gib
        pk[:, PK_LAM:PK_LAM + 16] = lam
        pk[:, PK_H0:PK_H0 + 16] = st[b, 0].reshape(2, 8, 128).transpose(2, 0, 1).reshape(128, 16)
        pk[:, PK_SEL] = 1.0 if half == 0 else 0.0
        pk[:, PK_SEL + 1] = 0.0 if half == 0 else 1.0
        pk[:, PK_IDENT:PK_IDENT + 128] = np.eye(128, dtype=np.float32)
        m = dict(shared)
        m["pk"] = pk
        m["x_own"] = np.ascontiguousarray(np.concatenate(
            [x_prompt[2 * k], x_prompt[2 * k + 1], x_sample[b, half * 1024:(half + 1) * 1024]], axis=0))
        m["x_oth"] = np.ascontiguousarray(x_sample[b, (1 - half) * 1024:(2 - half) * 1024])
        maps.append(m)
    return maps


def assemble(results):
    y_prompt = np.zeros((16, 256, DM), np.float32)
    y_sample = np.zeros((4, 2048, DM), np.float32)
    new_state = np.zeros((16, 1, 2, 1024), np.float32)
    for k in range(8):
        b, half = k // 2, k % 2
        y = np.asarray(results[k]["y_own"], np.float32)
        y_prompt[2 * k] = y[0:256]
        y_prompt[2 * k + 1] = y[256:512]
        y_sample[b, half * 1024:(half + 1) * 1024] = y[512:1536]
        stt = np.asarray(results[k]["st"], np.float32).reshape(128, 2, 2, 8)
        for sq in range(2):
            new_state[2 * k + sq, 0] = stt[:, sq].transpose(1, 2, 0).reshape(2, 1024)
    return y_prompt, y_sample, new_state


def kernel(**inputs):
    nc = build_nc()
    maps = make_in_maps(inputs)
    res = run_bass_kernel_spmd(nc, maps, core_ids=list(range(8)))
    return assemble(res.results)
```

```python
import numpy as np
from contextlib import ExitStack
import concourse.bass as bass
import concourse.mybir as mybir
from concourse.bass_utils import run_bass_kernel_spmd

F32 = mybir.dt.float32
BF16 = mybir.dt.bfloat16
AF = mybir.ActivationFunctionType
ALU = mybir.AluOpType

DM = 2048
NOWN = 1536
NOTH = 1024
EPOCH = 2000
SB_BASE = 16512
SB_END = 229344

PK_COND, PK_G1, PK_G2, PK_BADA, PK_CONVW, PK_CONVB = 0, 32, 48, 64, 160, 192
PK_GAB, PK_GIB, PK_LAM, PK_H0, PK_SEL, PK_IDENT, NPK = 200, 216, 232, 248, 264, 272, 400


class Buf:
    __slots__ = ("name", "w", "r", "dsem", "dcnt", "const")

    def __init__(self, name, const=False):
        self.name = name
        self.w = None
        self.r = {}
        self.dsem = None
        self.dcnt = 0
        self.const = const


class Tracker:
    ENGS = ("pe", "act", "dve", "pool", "sp")

    def __init__(self, nc, es):
        self.nc = nc
        self.es = es
        self.q = {e: [] for e in self.ENGS}
        self.sems = []
        self.cur = {}
        self.waited = {e: {} for e in self.ENGS}
        self.last = {}
        self.dma_toks = []

    def new_sem(self, name):
        h = self.es.enter_context(self.nc.semaphore(f"{name}{len(self.sems)}"))
        self.sems.append(h)
        return len(self.sems) - 1

    def _tick(self, eng):
        c = self.cur.get(eng)
        if c is None or c[1] >= EPOCH:
            c = [self.new_sem("c" + eng), 0]
            self.cur[eng] = c
        c[1] += 1
        tok = (c[0], c[1], eng)
        self.last[eng] = tok
        return tok

    def need(self, eng, tok):
        if tok is None:
            return
        sid, val, src = tok
        if eng == "pe" and src == "pe":
            return
        if self.waited[eng].get(sid, 0) >= val:
            return
        self.waited[eng][sid] = val
        h = self.sems[sid]
        self.q[eng].append(lambda E, h=h, v=val: E.wait_ge(h, v))

    def _deps(self, eng, reads, writes):
        for b in reads:
            self.need(eng, b.w)
        for b in writes:
            self.need(eng, b.w)
            for t in b.r.values():
                self.need(eng, t)

    def _mark(self, tok, reads, writes):
        for b in reads:
            if not b.const:
                b.r[tok[0]] = tok
        for b in writes:
            b.w = tok
            b.r = {}

    def op(self, eng, fn, reads=(), writes=()):
        self._deps(eng, reads, writes)
        tok = self._tick(eng)
        h = self.sems[tok[0]]
        self.q[eng].append(lambda E, fn=fn, h=h: fn(E).then_inc(h, 1))
        self._mark(tok, reads, writes)
        return tok

    def mm(self, mms, reads, writes, transpose=False):
        self._deps("pe", reads, writes)
        tok = self._tick("pe")
        h = self.sems[tok[0]]

        def thunk(E, mms=mms, h=h):
            ins = None
            for m in mms:
                if transpose:
                    ins = E.transpose(out=m[0], in_=m[1], identity=m[2])
                else:
                    ins = E.matmul(m[0], lhsT=m[1], rhs=m[2], start=m[3], stop=m[4])
            ins.then_inc(h, 1)

        self.q["pe"].append(thunk)
        self._mark(tok, reads, writes)
        return tok

    def dma(self, q, out, in_, semb, reads=(), writes=()):
        self._deps(q, reads, writes)
        if semb.dsem is None:
            semb.dsem = self.new_sem("d")
        semb.dcnt += 16
        tok = (semb.dsem, semb.dcnt, "dma")
        h = self.sems[semb.dsem]
        self.q[q].append(lambda E, o=out, i=in_, h=h: E.dma_start(out=o, in_=i).then_inc(h, 16))
        self._mark(tok, reads, writes)
        self.dma_toks.append(tok)
        return tok

    def barrier(self):
        toks = list(self.last.values()) + self.dma_toks
        for e in self.ENGS:
            for t in toks:
                self.need(e, t)
        self.dma_toks = []

    def finish(self):
        for t in self.dma_toks:
            self.need("sp", t)
        for t in self.last.values():
            self.need("sp", t)


def rev(ap):
    n = ap.ap[-1][1]
    return bass.AP(ap.tensor, ap.offset + (n - 1), [[ap.ap[0][0], ap.ap[0][1]], [-1, n]])


def bc_mid(ap, k):
    return bass.AP(ap.tensor, ap.offset, [[ap.ap[0][0], ap.ap[0][1]], [0, k], [ap.ap[1][0], ap.ap[1][1]]])


def bc_last(ap, k):
    return bass.AP(ap.tensor, ap.offset, [[ap.ap[0][0], ap.ap[0][1]], [ap.ap[1][0], ap.ap[1][1]], [0, k]])


def build_nc(debug=False):
    nc = bass.Bass("TRN2", target_bir_lowering=False)
    es = ExitStack()

    def din(name, shape):
        return nc.dram_tensor(name, list(shape), F32, kind="ExternalInput").ap()

    x_own = din("x_own", [NOWN, DM])
    x_oth = din("x_oth", [NOTH, DM])
    pk_d = din("pk", [128, NPK])
    ident_d = din("ident", [128, 128])
    sgub_d = din("sgub", [8, 128])
    fg_d = din("fg", [1, DM])
    w_ada = din("w_ada", [DM, 6 * DM])
    w_in = din("w_in", [DM, 4096])
    w_out = din("w_out", [DM, DM])
    w_ff1 = din("w_ff1", [DM, 8192])
    w_ff2 = din("w_ff2", [8192, DM])
    ga_w = din("ga_w", [16, 128, 128])
    gi_w = din("gi_w", [16, 128, 128])
    sgu_wT = din("sgu_wT", [8, 128, 128])
    y_own = nc.dram_tensor("y_own", [NOWN, DM], F32, kind="ExternalOutput").ap()
    st_d = nc.dram_tensor("st", [128, 32], F32, kind="ExternalOutput").ap()
    x1s = nc.dram_tensor("x1s", [NOWN, DM], F32, kind="Internal").ap()
    dbg = {}
    if debug:
        dbg["modT"] = nc.dram_tensor("dbg_modT", [128, 192], F32, kind="ExternalOutput").ap()
        dbg["hsum"] = nc.dram_tensor("dbg_hsum", [128, 8 * NOWN], F32, kind="ExternalOutput").ap()
        dbg["hT"] = nc.dram_tensor("dbg_hT", [128, 16 * NOWN], BF16, kind="ExternalOutput").ap()
        dbg["cat"] = nc.dram_tensor("dbg_cat", [128, 16 * NOWN], BF16, kind="ExternalOutput").ap()

    T = Tracker(nc, es)

    pos = [SB_BASE]
    cnt = [0]

    def at(off, shape, dt):
        cnt[0] += 1
        return nc.alloc_sbuf_tensor_at(f"t{cnt[0]}", list(shape), dt, offset=off)

    def bump(nbytes):
        o = pos[0]
        pos[0] += (nbytes + 31) // 32 * 32
        return o

    Z0 = bump(49152)
    Z1 = bump(49152)
    Z2 = bump(40960)
    WS = [bump(16384) for _ in range(3)]

    def pers(shape, dt):
        n = int(np.prod(shape[1:])) * (4 if dt == F32 else 2)
        return at(bump(n), shape, dt)

    pk = pers([128, NPK], F32)
    modT = pers([128, 192], F32)
    s1 = pers([128, 32], F32)
    s2 = pers([128, 32], F32)
    sc_b = pers([128, 32], BF16)
    identb = pers([128, 128], BF16)
    ones_f = pers([128, 128], F32)
    GAW_OFF = pos[0]
    gaw_b = pers([128, 16, 128], BF16)
    giw_b = pers([128, 16, 128], BF16)
    cst = pers([128, 8], F32)
    c1h = pers([128, 16], F32)
    c1f = pers([128, 16], F32)
    hba = pers([128, 16], F32)
    hbi = pers([128, 16], F32)
    sptmp = pers([128, 16], F32)
    stat_ss = pers([128, 32], F32)
    stat_sq = pers([128, 32], F32)
    stat_rs = pers([128, 32], F32)
    inits = pers([128, 16], F32)
    itmp = pers([128, 16], F32)
    bown = pers([128, 8, 4], F32)
    both = pers([128, 8, 4], F32)
    st_t = pers([128, 32], F32)
    WST_OFF = pos[0]
    wsT_b = pers([128, 8, 128], BF16)
    sgub_bc = pers([128, 8, 128], F32)
    assert pos[0] <= SB_END, pos[0]

    wslot = [at(WS[i], [128, 16, 512], BF16) for i in range(3)]
    wslot_b = [Buf(f"ws{i}") for i in range(3)]

    tp = [nc.alloc_psum_tensor(f"tp{i}", [128, 1024], BF16) for i in range(2)]
    tp_b = [Buf(f"tp{i}") for i in range(2)]
    mT = nc.alloc_psum_tensor("mT", [128, 512], F32)
    _mT_all = Buf("mT")
    mT_b = [_mT_all] * 6
    NG = 5
    pg = [nc.alloc_psum_tensor(f"pg{i}", [128, 512], F32) for i in range(NG)]
    pg_b = [Buf(f"pg{i}") for i in range(NG)]
    pgi = [0]

    def nextpg():
        i = pgi[0] % NG
        pgi[0] += 1
        return pg[i], pg_b[i]

    pk_b = Buf("pk")
    cb = Buf("cb")
    modT_b = [Buf(f"mod{v}") for v in range(6)]
    s1_b, s2_b = Buf("s1"), Buf("s2")

    def pkc(c0, n):
        return pk[:, c0:c0 + n]

    def act(out, in_, func, reads, writes, bias=None, scale=None, accum=None):
        def fn(E):
            kw = {}
            if bias is not None:
                kw["bias"] = bias
            if scale is not None:
                kw["scale"] = scale
            if accum is not None:
                kw["accum_out"] = accum
            return E.activation(out=out, in_=in_, func=func, **kw)
        return T.op("act", fn, reads, writes)

    def ts(out, in0, s1_, s2_, op0, op1, reads, writes, eng="dve"):
        if s2_ is None:
            return T.op(eng, lambda E: E.tensor_scalar(out=out, in0=in0, scalar1=s1_, scalar2=None, op0=op0), reads, writes)
        return T.op(eng, lambda E: E.tensor_scalar(out=out, in0=in0, scalar1=s1_, scalar2=s2_, op0=op0, op1=op1), reads, writes)

    def tt(out, in0, in1, op, reads, writes, eng="dve"):
        return T.op(eng, lambda E: E.tensor_tensor(out=out, in0=in0, in1=in1, op=op), reads, writes)

    def stt(out, in0, scalar, in1, op0, op1, reads, writes):
        return T.op("dve", lambda E: E.scalar_tensor_tensor(out=out, in0=in0, scalar=scalar, in1=in1, op0=op0, op1=op1), reads, writes)

    def scan(out, d0, d1, init, reads, writes):
        return T.op("dve", lambda E: E.tensor_tensor_scan(out=out, data0=d0, data1=d1, initial=init, op0=ALU.mult, op1=ALU.add), reads, writes)

    def colbc(col_ap, n):
        return bass.AP(col_ap.tensor, col_ap.offset, [[col_ap.ap[0][0], col_ap.ap[0][1]], [0, n]])

    def memset(ap, val, writes, eng="dve"):
        return T.op(eng, lambda E: E.memset(ap, val), (), writes)

    def wsrc(w, r0, c0):
        return w[r0:r0 + 2048, c0:c0 + 512].rearrange("(c p) n -> p c n", p=128)

    stream = []
    for jb in range(8):
        stream.append(("ada", jb, wsrc(w_ada, 0, jb * 512)))
    for blk in (0, 1, 2, 3):
        stream.append(("in", blk, wsrc(w_in, 0, blk * 512)))
    for jb in range(8, 24):
        stream.append(("ada", jb, wsrc(w_ada, 0, jb * 512)))
    for blk in (6, 7, 4, 5):
        stream.append(("in", blk, wsrc(w_in, 0, blk * 512)))
    for db in range(4):
        stream.append(("out", db, wsrc(w_out, 0, db * 512)))
    for g in range(3):
        for fb in range(16):
            stream.append(("ff1", fb, wsrc(w_ff1, 0, fb * 512)))
        for db in range(4):
            for ks in range(4):
                stream.append(("ff2", (db, ks), wsrc(w_ff2, ks * 2048, db * 512)))
    st_issue = [0]
    st_take = [0]
    free_slots = [0, 1, 2]
    loaded = {}

    def ws_issue():
        while free_slots and st_issue[0] < len(stream):
            s = free_slots.pop(0)
            n = st_issue[0]
            st_issue[0] += 1
            T.dma("pool", wslot[s][:, :, :], stream[n][2], wslot_b[s], (), [wslot_b[s]])
            loaded[n] = s

    def ws_acquire(kind, key):
        n = st_take[0]
        assert stream[n][0] == kind and stream[n][1] == key, (stream[n][:2], kind, key)
        st_take[0] += 1
        ws_issue()
        assert n in loaded, "weight ring deadlock"
        return loaded.pop(n)

    def ws_release(s):
        free_slots.append(s)
        ws_issue()

    T.dma("sp", pk[:, :], pk_d[:, :], pk_b, (), [pk_b])
    sg_b = Buf("sgub")
    ci_b, cg_b, ch_b, cw_b = Buf("ci"), Buf("cg"), Buf("ch"), Buf("cw")
    T.dma("pool", identb[:, :], ident_d[:, :], ci_b, (), [ci_b])
    T.dma("pool", gaw_b[:, :, :], ga_w.rearrange("g i j -> i g j"), cg_b, (), [cg_b])
    T.dma("pool", giw_b[:, :, :], gi_w.rearrange("g i j -> i g j"), ch_b, (), [ch_b])
    ws_issue()
    ms_b = Buf("ms")
    memset(ones_f[:, :], 1.0, [ms_b])
    memset(cst[:, 0:1], 1e-6, [ms_b])
    memset(cst[:, 1:2], 0.25, [ms_b])
    memset(cst[:, 2:3], 1.0, [ms_b])
    memset(cst[:, 3:4], 0.0, [ms_b])
    memset(cst[:, 4:5], -0.25, [ms_b])
    memset(st_t[:, :], 0.0, [ms_b])
    identf = pkc(PK_IDENT, 128)

    scb_b = Buf("scb")
    act(sc_b[:, :], pkc(PK_COND, 32), AF.Silu, [pk_b], [scb_b])

    def mod_block(jb):
        s = ws_acquire("ada", jb)
        v = jb // 4
        for js in range(4):
            jj = jb * 4 + js
            mms = [(mT[:, jj * 2:jj * 2 + 2], wslot[s][:, c, js * 128:(js + 1) * 128], sc_b[:, 2 * c:2 * c + 2], c == 0, c == 15)
                   for c in range(16)]
            T.mm(mms, [wslot_b[s], scb_b], [mT_b[v]])
        ws_release(s)
        if jb % 4 == 3:
            o = modT[:, v * 32:(v + 1) * 32].rearrange("p (c e) -> p c e", e=2)
            i0 = mT[:, v * 32:(v + 1) * 32].rearrange("p (c e) -> p c e", e=2)
            i1 = bc_last(pkc(PK_BADA + v * 16, 16), 2)
            tt(o, i0, i1, ALU.add, [mT_b[v], pk_b], [modT_b[v]])
            if v == 1:
                stt(s1[:, :].rearrange("p (c e) -> p c e", e=2), modT[:, 32:64].rearrange("p (c e) -> p c e", e=2), 1.0,
                    bc_last(pkc(PK_G1, 16), 2), ALU.add, ALU.mult, [modT_b[1], pk_b], [s1_b])
            if v == 4:
                stt(s2[:, :].rearrange("p (c e) -> p c e", e=2), modT[:, 128:160].rearrange("p (c e) -> p c e", e=2), 1.0,
                    bc_last(pkc(PK_G2, 16), 2), ALU.add, ALU.mult, [modT_b[4], pk_b], [s2_b])

    def norm_p1(xt_ap, xt_b, junk_ap, junk_b, xn_ap, xn_b, col):
        ss = stat_ss[:, col:col + 1]
        sq = stat_sq[:, col:col + 1]
        rs = stat_rs[:, col:col + 1]
        stb = Buf("st")
        act(junk_ap, xt_ap, AF.Square, [xt_b], [junk_b, stb], accum=ss)
        act(sq, ss, AF.Sqrt, [stb, ms_b], [stb], bias=cst[:, 0:1], scale=1.0 / DM)
        T.op("dve", lambda E: E.reciprocal(out=rs, in_=sq), [stb], [stb])
        ts(xn_ap, xt_ap, rs, None, ALU.mult, None, [xt_b, stb, junk_b], [xn_b])

    def norm_p2(xn_ap, xn_b, sc_t, sc_buf, sh_ap_fn, sh_buf, e, dst_fn, dst_b):
        for hf in range(2):
            mms = [(tp[hf][:, k * 128:(k + 1) * 128], xn_ap[:, (hf * 8 + k) * 128:(hf * 8 + k + 1) * 128], identb[:, :]) for k in range(8)]
            T.mm(mms, [xn_b, ci_b], [tp_b[hf]], transpose=True)
            for k in range(8):
                c = hf * 8 + k
                src = tp[hf][:, k * 128:(k + 1) * 128]
                scl = sc_t[:, 2 * c + e:2 * c + e + 1]
                shf = sh_ap_fn(c, e)
                if k % 2 == 0:
                    act(dst_fn(c), src, AF.Identity, [tp_b[hf], sc_buf, sh_buf], [dst_b], bias=shf, scale=scl)
                else:
                    ts(dst_fn(c), src, scl, shf, ALU.mult, ALU.add, [tp_b[hf], sc_buf, sh_buf], [dst_b])

    def norm_tile(xt_ap, xt_b, junk_ap, junk_b, xn_ap, xn_b, col, sc_t, sc_buf, sh_ap_fn, sh_buf, e, dst_fn, dst_b):
        norm_p1(xt_ap, xt_b, junk_ap, junk_b, xn_ap, xn_b, col)
        norm_p2(xn_ap, xn_b, sc_t, sc_buf, sh_ap_fn, sh_buf, e, dst_fn, dst_b)

    hT_own = at(Z0, [128, 16, NOWN], BF16)
    hT_oth = at(Z1, [128, 16, NOTH], BF16)
    hT_b = [Buf(f"hT{i}") for i in range(20)]
    xt = [at(Z0 + i * 8192, [128, DM], F32) for i in range(3)]
    xt_b = [Buf(f"xt{i}") for i in range(3)]
    junk = at(Z0 + 24576, [128, DM], BF16)
    junk_b = Buf("junk")
    xns = [at(Z1 + 8192 + i * 4096, [128, DM], BF16) for i in range(20)]
    xns_b = [Buf(f"xns{i}") for i in range(20)]

    def hT_ap(c, t0, n):
        if t0 < NOWN:
            assert t0 + n <= NOWN
            return hT_own[:, c, t0:t0 + n]
        return hT_oth[:, c, t0 - NOWN:t0 - NOWN + n]

    def n1_load(i):
        src = x_own[i * 128:(i + 1) * 128, :] if i < 12 else x_oth[(i - 12) * 128:(i - 11) * 128, :]
        T.dma("sp", xt[i % 3][:, :], src, xt_b[i % 3], (), [xt_b[i % 3]])

    n1_load(0)
    n1_load(1)
    for i in range(20):
        if i + 2 < 20:
            n1_load(i + 2)
        norm_p1(xt[i % 3][:, :], xt_b[i % 3], junk[:, :], junk_b, xns[i][:, :], xns_b[i], i)
    for jb in range(8):
        mod_block(jb)
    T.barrier()
    for i in range(20):
        e = 0 if i < 4 else 1
        norm_p2(xns[i][:, :], xns_b[i], s1, s1_b, lambda c, e: modT[:, 2 * c + e:2 * c + e + 1], modT_b[0], e,
                lambda c, i=i: hT_ap(c, i * 128, 128), hT_b[i])
    T.barrier()

    gy = at(Z1 + 65536, [128, 8, NOWN], BF16)
    gy_b = [Buf(f"gy{h}") for h in range(8)]
    for blk in (0, 1):
        s = ws_acquire("in", blk)
        for hh in range(4):
            h = blk * 4 + hh
            for tb in range(3):
                p_, pb_ = nextpg()
                mms = [(p_[:, :], wslot[s][:, c, hh * 128:(hh + 1) * 128], hT_own[:, c, tb * 512:(tb + 1) * 512], c == 0, c == 15) for c in range(16)]
                T.mm(mms, [wslot_b[s]] + hT_b[tb * 4:tb * 4 + 4], [pb_])
                act(gy[:, h, tb * 512:(tb + 1) * 512], p_[:, :], AF.Gelu_apprx_tanh, [pb_], [gy_b[h]])
        ws_release(s)

    prm_b = Buf("prm")
    act(sptmp[:, :], pkc(PK_LAM, 16), AF.Exp, [pk_b], [prm_b], scale=-1.0)
    act(sptmp[:, :], sptmp[:, :], AF.Ln, [prm_b, ms_b], [prm_b], bias=cst[:, 2:3])
    z16 = colbc(cst[:, 3:4], 16)
    stt(c1h[:, :], sptmp[:, :], -4.0, z16, ALU.mult, ALU.add, [prm_b, ms_b], [prm_b])
    stt(c1f[:, :], sptmp[:, :], -8.0, z16, ALU.mult, ALU.add, [prm_b, ms_b], [prm_b])
    stt(hba[:, :], pkc(PK_GAB, 16), 0.5, z16, ALU.mult, ALU.add, [pk_b, ms_b], [prm_b])
    stt(hbi[:, :], pkc(PK_GIB, 16), 0.5, z16, ALU.mult, ALU.add, [pk_b, ms_b], [prm_b])

    s_x2 = ws_acquire("in", 2)
    s_x3 = ws_acquire("in", 3)

    def xw(h, c):
        s = s_x2 if h < 4 else s_x3
        return wslot[s][:, c, (h % 4) * 128:(h % 4 + 1) * 128], wslot_b[s]

    sel0 = pkc(PK_SEL, 1)
    sel1 = pkc(PK_SEL + 1, 1)

    class Unit:
        pass

    def mk_unit(xr_off, xc_offs, xcb_off, set_offs, width, ncomp):
        u = Unit()
        wb = (width * 4 + 31) // 32 * 32
        u.W = width
        u.xr = at(xr_off, [128, width], F32)
        u.xc = [at(o, [128, width], F32) for o in xc_offs]
        u.xcb = at(xcb_off, [128, ncomp], BF16)
        u.xr_b, u.xcb_b = Buf("xr"), Buf("xcb")
        u.xc_b = [Buf("xc0"), Buf("xc1")]
        u.TA = [at(so, [128, width], F32) for so in set_offs]
        u.TI = [at(so + wb, [128, width], F32) for so in set_offs]
        u.MH = [at(so + 2 * wb, [128, width], F32) for so in set_offs]
        u.TA_b = [Buf("TA") for _ in set_offs]
        u.TI_b = [Buf("TI") for _ in set_offs]
        u.MH_b = [Buf("MH") for _ in set_offs]
        memset(u.xr[:, :], 0.0, [u.xr_b])
        for d in range(2):
            memset(u.xc[d][:, :], 0.0, [u.xc_b[d]])
            memset(u.TA[d][:, :], 0.0, [u.TA_b[d]])
            memset(u.TI[d][:, :], 0.0, [u.TI_b[d]])
            memset(u.MH[d][:, :], 0.0, [u.MH_b[d]])
        return u

    def rg_conv(u, h):
        xr, xc, xc_b = u.xr, u.xc[h % 2], u.xc_b[h % 2]
        lo, hi = 2, u.W - 2
        stt(xc[:, lo:hi], xr[:, lo - 2:hi - 2], pkc(PK_CONVW + h * 4, 1), colbc(pkc(PK_CONVB + h, 1), hi - lo), ALU.mult, ALU.add,
            [u.xr_b, pk_b], [xc_b])
        for k in (1, 2, 3):
            stt(xc[:, lo:hi], xr[:, lo - 2 + k:hi - 2 + k], pkc(PK_CONVW + h * 4 + k, 1), xc[:, lo:hi], ALU.mult, ALU.add,
                [u.xr_b, pk_b, xc_b], [xc_b])

    def rg_xcb(u, h, segs):
        for (ps_, ln, cs) in segs:
            act(u.xcb[:, cs:cs + ln], u.xc[h % 2][:, ps_:ps_ + ln], AF.Identity, [u.xc_b[h % 2]], [u.xcb_b])

    def rg_te(u, h, segs):
        lo, hi = 2, u.W - 2
        for d in range(2):
            gi = d * 8 + h
            TA, TI, MH = u.TA[d], u.TI[d], u.MH[d]
            for (ps_, ln, cs) in segs:
                for o in range(0, ln, 512):
                    m = min(512, ln - o)
                    for (gw, hb_, dst, dst_b) in ((gaw_b, hba, TA, u.TA_b[d]), (giw_b, hbi, TI, u.TI_b[d])):
                        p_, pb_ = nextpg()
                        T.mm([(p_[:, 0:m], gw[:, gi, :], u.xcb[:, cs + o:cs + o + m], True, True)], [u.xcb_b, cg_b, ch_b], [pb_])
                        act(dst[:, ps_ + o:ps_ + o + m], p_[:, 0:m], AF.Tanh, [pb_, prm_b], [dst_b],
                            bias=hb_[:, gi:gi + 1], scale=0.5)
            act(MH[:, lo:hi], TA[:, lo:hi], AF.Exp, [u.TA_b[d], prm_b], [u.MH_b[d]], bias=c1f[:, gi:gi + 1], scale=c1f[:, gi:gi + 1])
            act(TA[:, lo:hi], TA[:, lo:hi], AF.Exp, [u.TA_b[d], prm_b], [u.TA_b[d]], bias=c1h[:, gi:gi + 1], scale=c1h[:, gi:gi + 1])

    def rg_tail(u, h, segs, init_f, init_b, init_bufs, out_f, out_b, outf_b, outb_b):
        lo, hi = 2, u.W - 2
        xc, xc_b = u.xc[h % 2], u.xc_b[h % 2]
        for d in range(2):
            stt(u.MH[d][:, lo:hi], u.MH[d][:, lo:hi], 1.0, colbc(cst[:, 4:5], hi - lo), ALU.min, ALU.mult, [u.MH_b[d], ms_b], [u.MH_b[d]])
        for d in range(2):
            act(u.MH[d][:, lo:hi], u.MH[d][:, lo:hi], AF.Sqrt, [u.MH_b[d], ms_b], [u.MH_b[d]], bias=cst[:, 1:2], scale=1.0)
        for d in range(2):
            TA, TI, MH = u.TA[d], u.TI[d], u.MH[d]
            stt(TI[:, lo:hi], TI[:, lo:hi], 1.0, xc[:, lo:hi], ALU.add, ALU.mult, [u.TI_b[d], xc_b], [u.TI_b[d]])
            tt(TI[:, lo:hi], TI[:, lo:hi], MH[:, lo:hi], ALU.mult, [u.TI_b[d], u.MH_b[d]], [u.TI_b[d]])
            for si, (ps_, ln, cs) in enumerate(segs):
                a_ = TA[:, ps_:ps_ + ln]
                b_ = TI[:, ps_:ps_ + ln]
                if d == 0:
                    scan(out_f[:, cs:cs + ln], a_, b_, init_f[si], [u.TA_b[d], u.TI_b[d]] + init_bufs, [outf_b])
                else:
                    scan(rev(out_b[:, cs:cs + ln]), rev(a_), rev(b_), init_b[si], [u.TA_b[d], u.TI_b[d]] + init_bufs, [outb_b])

    def dcopy(out, in_, reads, writes):
        n = out.ap[-1][1]
        return tt(out, in_, colbc(cst[:, 3:4], n), ALU.add, list(reads) + [ms_b], writes)

    OX = Z1 + 32768
    SP0 = pos[0]
    uo = mk_unit(WST_OFF, [OX, WST_OFF + 6176], WST_OFF + 4128, [OX + 4128, OX + 4128 * 4], 1028, 1024)
    assert OX + 4128 * 7 <= Z1 + 65536 and WST_OFF + 6176 + 4128 <= SB_END
    bnd_b = Buf("bnd")
    carry_b = Buf("carry")
    segs_o = [(2, 1024, 0)]

    def oth_front(h):
        p_, pb_ = nextpg()
        mms = []
        for gi_, t0 in enumerate((512, 512 + 1022)):
            for c in range(16):
                mms.append((p_[:, gi_ * 2:gi_ * 2 + 2], xw(h, c)[0], hT_own[:, c, t0:t0 + 2], c == 0, c == 15))
        T.mm(mms, [wslot_b[s_x2], wslot_b[s_x3], hT_b[4], hT_b[11]], [pb_])
        act(bown[:, h, :], p_[:, 0:4], AF.Identity, [pb_], [bnd_b])
        for tb in range(2):
            p_, pb_ = nextpg()
            mms = [(p_[:, :], xw(h, c)[0], hT_oth[:, c, tb * 512:(tb + 1) * 512], c == 0, c == 15) for c in range(16)]
            T.mm(mms, [wslot_b[s_x2], wslot_b[s_x3]] + hT_b[12 + tb * 4:16 + tb * 4], [pb_])
            act(uo.xr[:, 2 + tb * 512:2 + (tb + 1) * 512], p_[:, :], AF.Identity, [pb_], [uo.xr_b])
        stt(uo.xr[:, 0:2], bown[:, h, 2:4], sel0, colbc(cst[:, 3:4], 2), ALU.mult, ALU.add, [bnd_b, pk_b, ms_b], [uo.xr_b])
        stt(uo.xr[:, 1026:1027], bown[:, h, 0:1], sel1, colbc(cst[:, 3:4], 1), ALU.mult, ALU.add, [bnd_b, pk_b, ms_b], [uo.xr_b])
        dcopy(both[:, h, 0:1], uo.xr[:, 2:3], [uo.xr_b], [bnd_b])
        dcopy(both[:, h, 1:3], uo.xr[:, 1024:1026], [uo.xr_b], [bnd_b])
        rg_conv(uo, h)

    oth_front(0)
    rg_xcb(uo, 0, segs_o)
    for h in range(8):
        if h + 1 < 8:
            oth_front(h + 1)
        rg_te(uo, h, segs_o)
        if h + 1 < 8:
            rg_xcb(uo, h + 1, segs_o)
        rg_tail(uo, h, segs_o, [pkc(PK_H0 + h, 1)], [pkc(PK_H0 + 8 + h, 1)], [pk_b],
                uo.MH[0][:, 2:1026], uo.MH[1][:, 2:1026], uo.MH_b[0], uo.MH_b[1])
        stt(itmp[:, 2 * h:2 * h + 1], pkc(PK_H0 + h, 1), sel0, cst[:, 3:4], ALU.mult, ALU.add, [pk_b, ms_b], [carry_b])
        stt(inits[:, 2 * h:2 * h + 1], uo.MH[0][:, 1025:1026], sel1, itmp[:, 2 * h:2 * h + 1], ALU.mult, ALU.add, [uo.MH_b[0], carry_b, pk_b], [carry_b])
        stt(itmp[:, 2 * h + 1:2 * h + 2], pkc(PK_H0 + 8 + h, 1), sel1, cst[:, 3:4], ALU.mult, ALU.add, [pk_b, ms_b], [carry_b])
        stt(inits[:, 2 * h + 1:2 * h + 2], uo.MH[1][:, 2:3], sel0, itmp[:, 2 * h + 1:2 * h + 2], ALU.mult, ALU.add, [uo.MH_b[1], carry_b, pk_b], [carry_b])
        mod_block(8 + h)

    T.barrier()

    W = 1548
    uw = mk_unit(Z1 + 37248, [Z1 + 43456, WST_OFF], Z1 + 49664, [Z1, Z1 + 18624], W, NOWN)
    HF = at(Z1 + 52736, [128, NOWN], F32)
    HB = at(Z1 + 58880, [128, NOWN], F32)
    assert 58880 + 6144 <= 65536 and WST_OFF + 6208 <= SB_END
    HF_b, HB_b = Buf("HF"), Buf("HB")
    st_b = Buf("stt")
    segs_own = [(2, 256, 0), (262, 256, 256), (522, 1024, 512)]

    def own_front(h):
        for tb in range(3):
            p_, pb_ = nextpg()
            mms = [(p_[:, :], xw(h, c)[0], hT_own[:, c, tb * 512:(tb + 1) * 512], c == 0, c == 15) for c in range(16)]
            T.mm(mms, [wslot_b[s_x2], wslot_b[s_x3]] + hT_b[tb * 4:tb * 4 + 4], [pb_])
            if tb == 0:
                act(uw.xr[:, 2:258], p_[:, 0:256], AF.Identity, [pb_], [uw.xr_b])
                act(uw.xr[:, 262:518], p_[:, 256:512], AF.Identity, [pb_], [uw.xr_b])
            else:
                o = 522 + (tb - 1) * 512
                act(uw.xr[:, o:o + 512], p_[:, :], AF.Identity, [pb_], [uw.xr_b])
        stt(uw.xr[:, 520:522], both[:, h, 1:3], sel1, colbc(cst[:, 3:4], 2), ALU.mult, ALU.add, [bnd_b, pk_b, ms_b], [uw.xr_b])
        stt(uw.xr[:, 1546:1547], both[:, h, 0:1], sel0, colbc(cst[:, 3:4], 1), ALU.mult, ALU.add, [bnd_b, pk_b, ms_b], [uw.xr_b])
        rg_conv(uw, h)

    own_front(0)
    rg_xcb(uw, 0, segs_own)
    for h in range(8):
        if h + 1 < 8:
            own_front(h + 1)
        rg_te(uw, h, segs_own)
        if h + 1 < 8:
            rg_xcb(uw, h + 1, segs_own)
        rg_tail(uw, h, segs_own, [0.0, 0.0, inits[:, 2 * h:2 * h + 1]], [0.0, 0.0, inits[:, 2 * h + 1:2 * h + 2]], [carry_b],
                HF, HB, HF_b, HB_b)
        for sq_ in range(2):
            dcopy(st_t[:, (sq_ * 2) * 8 + h:(sq_ * 2) * 8 + h + 1], HF[:, sq_ * 256 + 255:sq_ * 256 + 256], [HF_b], [st_b])
            dcopy(st_t[:, (sq_ * 2 + 1) * 8 + h:(sq_ * 2 + 1) * 8 + h + 1], HB[:, sq_ * 256:sq_ * 256 + 1], [HB_b], [st_b])
        tt(HF[:, :], HF[:, :], HB[:, :], ALU.add, [HF_b, HB_b], [HF_b])
        tt(gy[:, h, :], gy[:, h, :], HF[:, :], ALU.mult, [gy_b[h], HF_b], [gy_b[h]])
        mod_block(16 + h)
    ws_release(s_x2)
    ws_release(s_x3)
    T.dma("sp", st_d[:, :], st_t[:, :], st_b, [st_b], [])
    rg_out = gy
    if debug:
        T.dma("sp", dbg["modT"][:, :], modT[:, :], modT_b[5], [modT_b[v] for v in range(6)], [])
        T.dma("sp", dbg["hT"][:, :], hT_own[:, :, :].rearrange("p c t -> p (c t)"), hT_b[0], hT_b[0:12], [])

    T.barrier()

    T.dma("sp", sgub_bc[:, :, :], bass.AP(sgub_d.tensor, 0, [[0, 128], [128, 8], [1, 128]]), sg_b, (), [sg_b])
    T.dma("pool", wsT_b[:, :, :], sgu_wT.rearrange("h q p -> q h p"), cw_b, (), [cw_b])
    gtmp = [at(Z2 + i * 2048, [128, 512], F32) for i in range(4)]
    gtmp_b = [Buf(f"gt{i}") for i in range(4)]
    gti = [0]

    def next_gt():
        i = gti[0] % 4
        gti[0] += 1
        return gtmp[i], gtmp_b[i]

    vtm = at(Z1, [128, 12, 1024], BF16)
    ch_out = at(Z1 + 24576, [128, 8, NOWN], BF16)
    vt_b = [Buf(f"vt{i}") for i in range(12)]
    ch_b2 = Buf("chout")
    s_v = [ws_acquire("in", 6), ws_acquire("in", 7)]
    for ci in range(12):
        for vb in range(2):
            p_, pb_ = nextpg()
            mms = [(p_[:, :], hT_own[:, c, ci * 128:(ci + 1) * 128], wslot[s_v[vb]][:, c, :], c == 0, c == 15) for c in range(16)]
            T.mm(mms, [wslot_b[s_v[vb]], hT_b[ci]], [pb_])
            act(vtm[:, ci, vb * 512:(vb + 1) * 512], p_[:, :], AF.Gelu_apprx_tanh, [pb_], [vt_b[ci]])
    ws_release(s_v[0])
    ws_release(s_v[1])
    for blk in (4, 5):
        s = ws_acquire("in", blk)
        for hh in range(4):
            h = (blk - 4) * 4 + hh
            for tb in range(3):
                p_, pb_ = nextpg()
                mms = [(p_[:, :], wslot[s][:, c, hh * 128:(hh + 1) * 128], hT_own[:, c, tb * 512:(tb + 1) * 512], c == 0, c == 15) for c in range(16)]
                T.mm(mms, [wslot_b[s]] + hT_b[tb * 4:tb * 4 + 4], [pb_])
                g_, gb_ = next_gt()
                act(g_[:, :], p_[:, :], AF.Gelu_apprx_tanh, [pb_], [gb_])
                pm, pmb = nextpg()
                mms = [(pm[:, k * 128:(k + 1) * 128], vtm[:, tb * 4 + k, h * 128:(h + 1) * 128], wsT_b[:, h, :], True, True) for k in range(4)]
                T.mm(mms, vt_b[tb * 4:tb * 4 + 4] + [cw_b], [pmb])
                m_, mb_ = next_gt()
                tt(m_[:, :].rearrange("p (k q) -> p k q", k=4), pm[:, :].rearrange("p (k q) -> p k q", k=4), bc_mid(sgub_bc[:, h, :], 4),
                   ALU.add, [pmb, sg_b], [mb_])
                tt(ch_out[:, h, tb * 512:(tb + 1) * 512], m_[:, :], g_[:, :], ALU.mult, [mb_, gb_], [ch_b2])
        ws_release(s)
    if debug:
        T.dma("sp", dbg["cat"][:, 0:8 * NOWN], rg_out[:, :, :].rearrange("p c t -> p (c t)"), gy_b[0], gy_b, [])
        T.dma("sp", dbg["cat"][:, 8 * NOWN:16 * NOWN], ch_out[:, :, :].rearrange("p c t -> p (c t)"), ch_b2, [ch_b2], [])

    T.barrier()

    def build_gate(v, e, diag, diag_b, gbc, gbc_b):
        for c in range(16):
            col = (v * 16 + c) * 2 + e
            ts(diag[:, c * 128:(c + 1) * 128], identf, modT[:, col:col + 1], None, ALU.mult, None, [pk_b, modT_b[v]], [diag_b])
        for q in range(4):
            p_, pb_ = nextpg()
            T.mm([(p_[:, :], ones_f[:, :], diag[:, q * 512:(q + 1) * 512], True, True)], [diag_b, ms_b], [pb_])
            act(gbc[:, q * 512:(q + 1) * 512], p_[:, :], AF.Identity, [pb_], [gbc_b])

    gbc1 = [at(Z0 + e * 8192, [128, DM], F32) for e in range(2)]
    gbc1_b = [Buf("gb1a"), Buf("gb1b")]
    diag1 = at(Z0 + 16384, [128, DM], F32)
    diag1_b = Buf("diag1")
    xblk = [at(Z0 + 24576 + i * 2048, [128, 512], F32) for i in range(4)]
    xblk_b = [Buf(f"xb{i}") for i in range(4)]
    oblk = [at(Z0 + 32768 + i * 2048, [128, 512], F32) for i in range(4)]
    oblk_b = [Buf(f"ob{i}") for i in range(4)]
    x1s_b = [[Buf(f"x1s{i}_{db}") for db in range(4)] for i in range(12)]
    for e in range(2):
        build_gate(2, e, diag1, diag1_b, gbc1[e], gbc1_b[e])

    jobs = [(db, i) for db in range(4) for i in range(12)]

    def b_load(n):
        db, i = jobs[n]
        T.dma("sp", xblk[n % 4][:, :], x_own[i * 128:(i + 1) * 128, db * 512:(db + 1) * 512], xblk_b[n % 4], (), [xblk_b[n % 4]])

    b_load(0)
    b_load(1)
    s = None
    for n, (db, i) in enumerate(jobs):
        if n + 2 < len(jobs):
            b_load(n + 2)
        if i == 0:
            s = ws_acquire("out", db)
        e = 0 if i < 4 else 1
        p_, pb_ = nextpg()
        mms = []
        for c in range(16):
            src = rg_out[:, c, i * 128:(i + 1) * 128] if c < 8 else ch_out[:, c - 8, i * 128:(i + 1) * 128]
            mms.append((p_[:, :], src, wslot[s][:, c, :], c == 0, c == 15))
        T.mm(mms, [wslot_b[s], ch_b2] + gy_b, [pb_])
        ob, obb = oblk[n % 4], oblk_b[n % 4]
        tt(ob[:, :], p_[:, :], gbc1[e][:, db * 512:(db + 1) * 512], ALU.mult, [pb_, gbc1_b[e]], [obb])
        tt(ob[:, :], ob[:, :], xblk[n % 4][:, :], ALU.add, [obb, xblk_b[n % 4]], [obb])
        T.dma("sp", x1s[i * 128:(i + 1) * 128, db * 512:(db + 1) * 512], ob[:, :], obb, [obb], [x1s_b[i][db]])
        if i == 11:
            ws_release(s)

    T.barrier()

    h2T = [at(Z0 + i * 16384, [128, 16, 512], BF16) for i in range(2)]
    h2T_b = [[Buf(f"h2T{i}_{t}") for t in range(4)] for i in range(2)]
    xq = [at(Z0 + 32768 + i * 8192, [128, DM], F32) for i in range(2)]
    xq_b = [Buf("xq0"), Buf("xq1")]
    xqi = [0]
    aT = at(Z1, [128, 64, 512], BF16)
    aT_b = [Buf(f"aT{i}") for i in range(64)]
    gbc2 = at(Z1 + 65536, [128, DM], F32)
    gbc2_b = Buf("gbc2")
    SCR = Z1 + 65536 + 8192
    diag2 = at(SCR, [128, DM], F32)
    xn2 = at(SCR, [128, DM], BF16)
    rtmp = [at(SCR + 4096 + i * 2048, [128, 512], F32) for i in range(2)]
    scr_b = Buf("scr")
    fg_bc = at(SCR + 8192, [128, DM], F32)
    fg_b = Buf("fg")
    assert SCR + 16384 <= Z2 + 40960
    SP0 = pos[0]
    xb = [at(GAW_OFF + i * 2048, [128, 512], F32) for i in range(4)]
    xb_b = [Buf(f"xb{i}") for i in range(4)]
    sqj = at(SP0, [128, 512], BF16)
    sqj_b = Buf("sqj")
    assert SP0 + 1024 <= SB_END
    _fst = Buf("fst")
    fstat_b = [_fst, _fst, _fst]
    T.dma("sp", fg_bc[:, :], bass.AP(fg_d.tensor, 0, [[0, 128], [1, DM]]), fg_b, (), [fg_b])

    def c_norm_p1(g, t_):
        i = g * 4 + t_
        q = xqi[0] % 2
        xqi[0] += 1
        T.dma("sp", xq[q][:, :], x1s[i * 128:(i + 1) * 128, :], xq_b[q], x1s_b[i], [xq_b[q]])
        norm_p1(xq[q][:, :], xq_b[q], xn2[:, :], scr_b, xn2[:, :], scr_b, 20 + t_)

    def c_norm_p2(g, t_):
        e = 0 if g == 0 else 1
        norm_p2(xn2[:, :], scr_b, s2, s2_b, lambda c, e: modT[:, 96 + 2 * c + e:96 + 2 * c + e + 1], modT_b[3], e,
                lambda c, t_=t_, g=g: h2T[g % 2][:, c, t_ * 128:(t_ + 1) * 128], h2T_b[g % 2][t_])

    def c_final(g, t_):
        i = g * 4 + t_
        q = xqi[0] % 2
        xqi[0] += 1
        T.dma("sp", xq[q][:, :], x1s[i * 128:(i + 1) * 128, :], xq_b[q], x1s_b[i], [xq_b[q]])
        stt(xq[q][:, :], xq[q][:, :], stat_rs[:, 8 + g * 4 + t_:8 + g * 4 + t_ + 1], fg_bc[:, :], ALU.mult, ALU.mult,
            [xq_b[q], fstat_b[g], fg_b], [xq_b[q]])
        T.dma("sp", y_own[i * 128:(i + 1) * 128, :], xq[q][:, :], xq_b[q], [xq_b[q]], [Buf("y")])

    for t_ in range(4):
        c_norm_p1(0, t_)
        c_norm_p2(0, t_)
    for g in range(3):
        e = 0 if g == 0 else 1
        if g < 2:
            build_gate(5, e, diag2, scr_b, gbc2, gbc2_b)
        for fb in range(16):
            s = ws_acquire("ff1", fb)
            for js in range(4):
                p_, pb_ = nextpg()
                mms = [(p_[:, :], wslot[s][:, c, js * 128:(js + 1) * 128], h2T[g % 2][:, c, :], c == 0, c == 15) for c in range(16)]
                T.mm(mms, [wslot_b[s]] + h2T_b[g % 2], [pb_])
                r_ = rtmp[(fb * 4 + js) % 2]
                act(r_[:, :], p_[:, :], AF.Relu, [pb_], [scr_b])
                tt(aT[:, fb * 4 + js, :], r_[:, :], r_[:, :], ALU.mult, [scr_b], [aT_b[fb * 4 + js]])
            ws_release(s)
            if g > 0 and fb in (2, 5, 8, 11):
                c_final(g - 1, (fb - 2) // 3)
        for db in range(4):
            if g < 2:
                c_norm_p1(g + 1, db)
            for t_ in range(4):
                i = g * 4 + t_
                T.dma("sp", xb[t_][:, :], x1s[i * 128:(i + 1) * 128, db * 512:(db + 1) * 512], xb_b[t_], [x1s_b[i][db]], [xb_b[t_]])
            banks = [nextpg() for _ in range(4)]
            for ks in range(4):
                s = ws_acquire("ff2", (db, ks))
                for t_ in range(4):
                    p_, pb_ = banks[t_]
                    mms = [(p_[:, :], aT[:, ks * 16 + c, t_ * 128:(t_ + 1) * 128], wslot[s][:, c, :], ks == 0 and c == 0, ks == 3 and c == 15)
                           for c in range(16)]
                    T.mm(mms, [wslot_b[s]] + aT_b[ks * 16:ks * 16 + 16], [pb_])
                ws_release(s)
            if g < 2:
                c_norm_p2(g + 1, db)
            for t_ in range(4):
                i = g * 4 + t_
                p_, pb_ = banks[t_]
                j = t_
                r_ = rtmp[t_ % 2]
                tt(r_[:, :], p_[:, :], gbc2[:, db * 512:(db + 1) * 512], ALU.mult, [pb_, gbc2_b], [scr_b])
                tt(xb[j][:, :], xb[j][:, :], r_[:, :], ALU.add, [scr_b, xb_b[j]], [xb_b[j]])
                act(sqj[:, :], xb[j][:, :], AF.Square, [xb_b[j]], [sqj_b, fstat_b[g]], accum=stat_ss[:, db * 4 + t_:db * 4 + t_ + 1])
                T.dma("sp", x1s[i * 128:(i + 1) * 128, db * 512:(db + 1) * 512], xb[j][:, :], xb_b[j], [xb_b[j]], [x1s_b[i][db]])
        tot = stat_sq[:, 0:4]
        tt(tot, stat_ss[:, 0:4], stat_ss[:, 4:8], ALU.add, [fstat_b[g]], [fstat_b[g]])
        tt(tot, tot, stat_ss[:, 8:12], ALU.add, [fstat_b[g]], [fstat_b[g]])
        tt(tot, tot, stat_ss[:, 12:16], ALU.add, [fstat_b[g]], [fstat_b[g]])
        act(tot, tot, AF.Sqrt, [fstat_b[g], ms_b], [fstat_b[g]], bias=cst[:, 0:1], scale=1.0 / DM)
        T.op("dve", lambda E, g=g, tot=tot: E.reciprocal(out=stat_rs[:, 8 + g * 4:12 + g * 4], in_=tot), [fstat_b[g]], [fstat_b[g]])
    for t_ in range(4):
        c_final(2, t_)
    assert st_take[0] == len(stream)
    T.finish()

    with nc.Block() as block:
        @block.sync
        def _(E):
            for f in T.q["sp"]:
                f(E)

        @block.gpsimd
        def _(E):
            for f in T.q["pool"]:
                f(E)

        @block.scalar
        def _(E):
            for f in T.q["act"]:
                f(E)

        @block.vector
        def _(E):
            for f in T.q["dve"]:
                f(E)

        @block.tensor
        def _(E):
            for f in T.q["pe"]:
                f(E)
    es.close()
    return nc


def _fm(v, n):
    return np.ascontiguousarray(np.asarray(v, np.float32).reshape(n, 128).T)


def make_in_maps(inp):
    x_prompt = np.asarray(inp["x_prompt"], np.float32)
    x_sample = np.asarray(inp["x_sample"], np.float32)
    c = np.asarray(inp["c"], np.float32)
    st = np.asarray(inp["state_rglru"], np.float32)
    shared = {
        "ident": np.eye(128, dtype=np.float32),
        "sgub": np.ascontiguousarray(np.asarray(inp["sgu_b"], np.float32)[0]),
        "fg": np.ascontiguousarray(np.asarray(inp["final_g"], np.float32).reshape(1, DM)),
        "w_ada": np.ascontiguousarray(np.asarray(inp["w_ada"], np.float32)[0]),
        "w_in": np.ascontiguousarray(np.asarray(inp["w_in"], np.float32)[0]),
        "w_out": np.ascontiguousarray(np.asarray(inp["w_out"], np.float32)[0]),
        "w_ff1": np.ascontiguousarray(np.asarray(inp["w_ff1"], np.float32)[0]),
        "w_ff2": np.ascontiguousarray(np.asarray(inp["w_ff2"], np.float32)[0]),
        "ga_w": np.ascontiguousarray(np.asarray(inp["ga_w"], np.float32)[0].reshape(16, 128, 128)),
        "gi_w": np.ascontiguousarray(np.asarray(inp["gi_w"], np.float32)[0].reshape(16, 128, 128)),
        "sgu_wT": np.ascontiguousarray(np.asarray(inp["sgu_w"], np.float32)[0].transpose(0, 2, 1)),
    }
    g1 = _fm(inp["norm1_g"][0], 16)
    g2 = _fm(inp["norm2_g"][0], 16)
    bada = _fm(inp["b_ada"][0], 96)
    convw = np.asarray(inp["conv_w"], np.float32)[0].reshape(4, 8, 128).transpose(2, 1, 0).reshape(128, 32)
    convb = _fm(inp["conv_b"][0], 8)
    gab = np.asarray(inp["ga_b"], np.float32)[0].reshape(2, 8, 128).transpose(2, 0, 1).reshape(128, 16)
    gib = np.asarray(inp["gi_b"], np.float32)[0].reshape(2, 8, 128).transpose(2, 0, 1).reshape(128, 16)
    lam = np.asarray(inp["lru_lambda"], np.float32)[0].reshape(2, 8, 128).transpose(2, 0, 1).reshape(128, 16)
    cctx = _fm(inp["c_ctx"], 16)
    maps = []
    for k in range(8):
        b, half = k // 2, k % 2
        pk = np.zeros((128, NPK), np.float32)
        cond = np.stack([cctx, _fm(c[b], 16)], axis=-1).reshape(128, 32)
        pk[:, PK_COND:PK_COND + 32] = cond
        pk[:, PK_G1:PK_G1 + 16] = g1
        pk[:, PK_G2:PK_G2 + 16] = g2
        pk[:, PK_BADA:PK_BADA + 96] = bada
        pk[:, PK_CONVW:PK_CONVW + 32] = convw
        pk[:, PK_CONVB:PK_CONVB + 8] = convb
        pk[:, PK_GAB:PK_GAB + 16] = gab
        pk[:, PK_GIB:PK_GIB + 16] = gib
        pk[:, PK_LAM:PK_LAM + 16] = lam
        pk[:, PK_H0:PK_H0 + 16] = st[b, 0].reshape(2, 8, 128).transpose(2, 0, 1).reshape(128, 16)
        pk[:, PK_SEL] = 1.0 if half == 0 else 0.0
        pk[:, PK_SEL + 1] = 0.0 if half == 0 else 1.0
        pk[:, PK_IDENT:PK_IDENT + 128] = np.eye(128, dtype=np.float32)
        m = dict(shared)
        m["pk"] = pk
        m["x_own"] = np.ascontiguousarray(np.concatenate(
            [x_prompt[2 * k], x_prompt[2 * k + 1], x_sample[b, half * 1024:(half + 1) * 1024]], axis=0))
        m["x_oth"] = np.ascontiguousarray(x_sample[b, (1 - half) * 1024:(2 - half) * 1024])
        maps.append(m)
    return maps


def assemble(results):
    y_prompt = np.zeros((16, 256, DM), np.float32)
    y_sample = np.zeros((4, 2048, DM), np.float32)
    new_state = np.zeros((16, 1, 2, 1024), np.float32)
    for k in range(8):
        b, half = k // 2, k % 2
        y = np.asarray(results[k]["y_own"], np.float32)
        y_prompt[2 * k] = y[0:256]
        y_prompt[2 * k + 1] = y[256:512]
        y_sample[b, half * 1024:(half + 1) * 1024] = y[512:1536]
        stt = np.asarray(results[k]["st"], np.float32).reshape(128, 2, 2, 8)
        for sq in range(2):
            new_state[2 * k + sq, 0] = stt[:, sq].transpose(1, 2, 0).reshape(2, 1024)
    return y_prompt, y_sample, new_state


def kernel(**inputs):
    nc = build_nc()
    maps = make_in_maps(inputs)
    res = run_bass_kernel_spmd(nc, maps, core_ids=list(range(8)))
    return assemble(res.results)
```
